# Optimizing a Trainium2 kernel written in Bass

```python
import functools
import jax, jax.numpy as jnp
from jax import lax
import numpy as np

D_MODEL = 1024
BATCH = 2
SEQ = 8192
DEPTH = 2
DEC_BATCH = 128
DEC_SEQ = 4
PAST_LEN = 16384
PAGE_SIZE = 128

HEAD_DIM = 64
MIX_WIDTH = D_MODEL
MEM_HEADS = 4
MEM_WIDTH = MEM_HEADS * HEAD_DIM
MAIN_WIDTH = MIX_WIDTH - MEM_WIDTH
RWKV_HEADS = MAIN_WIDTH // HEAD_DIM
DECAY_LORA = 64
AAA_LORA = 64
GATE_LORA = 128
RWKV_COLS = 3 * MAIN_WIDTH + DECAY_LORA + AAA_LORA + GATE_LORA
A_IN_COLS = RWKV_COLS + MEM_WIDTH
SWA_Q_HEADS = MAIN_WIDTH // HEAD_DIM
SWA_KV_HEADS = 4
SWA_GROUP = SWA_Q_HEADS // SWA_KV_HEADS
KV_WIDTH = SWA_KV_HEADS * HEAD_DIM
WINDOW = 128
BLOCK = 128
N_MEM = 256
D_FF = 4 * D_MODEL
N_A = DEPTH // 2
N_B = DEPTH - N_A
ROPE_THETA = 10000.0
NORM_EPS = 1e-6
LNX_EPS = 6.4e-4
L2_EPS = 1e-12
ATTN_SCALE = HEAD_DIM ** -0.5

kernel_name = "yoco_rwkv7_swa_sink_mem_step"


def rms_norm(x, g):
    xf = x.astype(jnp.float32)
    y = xf * lax.rsqrt(jnp.mean(xf * xf, axis=-1, keepdims=True) + NORM_EPS)
    return (y * g.astype(jnp.float32)).astype(x.dtype)


def rope(x, pos):
    half = HEAD_DIM // 2
    freqs = jnp.power(ROPE_THETA, -jnp.arange(half, dtype=jnp.float32) / half)
    ang = pos.astype(jnp.float32)[:, None] * freqs[None, :]
    cos, sin = jnp.cos(ang)[:, None, :], jnp.sin(ang)[:, None, :]
    xf = x.astype(jnp.float32)
    x1, x2 = xf[..., :half], xf[..., half:]
    return jnp.concatenate([x1 * cos - x2 * sin, x2 * cos + x1 * sin], axis=-1).astype(x.dtype)


def wkv_scan(r, w, k, v, a, b, S0):
    def step(S, inp):
        r_t, w_t, k_t, v_t, a_t, b_t = inp
        sa = jnp.einsum('bhij,bhj->bhi', S, a_t)
        S = S * w_t[:, :, None, :] + sa[..., None] * b_t[:, :, None, :] + v_t[..., None] * k_t[:, :, None, :]
        return S, jnp.einsum('bhij,bhj->bhi', S, r_t)
    xs = tuple(jnp.moveaxis(t, 1, 0) for t in (r, w, k, v, a, b))
    S, ys = lax.scan(step, S0, xs)
    return jnp.moveaxis(ys, 0, 1), S


def rwkv_time_mix(p, shift_prev, wkv_prev, mu, w_w2, w0, w_a2, a0, w_g2, k_k, k_a, r_k, lnx_w, lnx_b):
    B, T, _ = p.shape
    f32 = jnp.float32
    pf = p.astype(f32)
    prev = jnp.concatenate([shift_prev.astype(f32)[:, None], pf[:, :-1]], axis=1)
    ps = pf + (prev - pf) * mu.astype(f32)
    i1, i2, i3 = MAIN_WIDTH, 2 * MAIN_WIDTH, 3 * MAIN_WIDTH
    i4, i5 = i3 + DECAY_LORA, i3 + DECAY_LORA + AAA_LORA
    r, k, v, wd, ad, gd = jnp.split(ps, [i1, i2, i3, i4, i5], axis=-1)
    w_log = -jax.nn.softplus(-(w0.astype(f32) + jnp.tanh(wd) @ w_w2.astype(f32))) - 0.5
    decay = jnp.exp(-jnp.exp(w_log))
    a = jax.nn.sigmoid(a0.astype(f32) + ad @ w_a2.astype(f32))
    g = jax.nn.sigmoid(gd) @ w_g2.astype(f32)
    hs = lambda t: t.reshape(B, T, RWKV_HEADS, HEAD_DIM)
    kk = hs(k * k_k.astype(f32))
    kk = kk / jnp.maximum(jnp.linalg.norm(kk, axis=-1, keepdims=True), L2_EPS)
    k = k * (1.0 + (a - 1.0) * k_a.astype(f32))
    r_h, k_h, v_h, a_h = hs(r), hs(k), hs(v), hs(a)
    y, wkv = wkv_scan(r_h, hs(decay), k_h, v_h, -kk, kk * a_h, wkv_prev.astype(f32))
    mean = jnp.mean(y, axis=-1, keepdims=True)
    var = jnp.mean(jnp.square(y - mean), axis=-1, keepdims=True)
    y = ((y - mean) * lax.rsqrt(var + LNX_EPS)).reshape(B, T, MAIN_WIDTH) * lnx_w.astype(f32) + lnx_b.astype(f32)
    bonus = jnp.sum(r_h * k_h * r_k.astype(f32), axis=-1, keepdims=True) * v_h
    out = (y + bonus.reshape(B, T, MAIN_WIDTH)) * g
    return out.astype(p.dtype), p[:, -1], wkv.astype(wkv_prev.dtype)


def memory_kv(mem, g_norm, w_kv, g_knorm):
    B, M, _ = mem.shape
    kv = rms_norm(mem, g_norm) @ w_kv
    k = rms_norm(kv[..., :MEM_WIDTH].reshape(B, M, MEM_HEADS, HEAD_DIM), g_knorm)
    v = kv[..., MEM_WIDTH:].reshape(B, M, MEM_HEADS, HEAD_DIM)
    return k, v


def memory_attention(q, g_qnorm, mem_k, mem_v):
    B, T, _ = q.shape
    qh = rms_norm(q.reshape(B, T, MEM_HEADS, HEAD_DIM), g_qnorm)
    s = jnp.einsum('bthd,bmhd->bhtm', qh, mem_k, preferred_element_type=jnp.float32) * ATTN_SCALE
    p = jax.nn.softmax(s, axis=-1).astype(mem_v.dtype)
    return jnp.einsum('bhtm,bmhd->bthd', p, mem_v).reshape(B, T, MEM_WIDTH)


def sink_attention(q, k, v, mask, sinks):
    s = jnp.einsum('...qhgd,...khd->...hgqk', q, k, preferred_element_type=jnp.float32) * ATTN_SCALE
    s = jnp.where(mask[..., None, None, :, :], s, -jnp.inf)
    sink = sinks.astype(jnp.float32).reshape(SWA_KV_HEADS, SWA_GROUP, 1, 1)
    m = jnp.maximum(jnp.max(s, axis=-1, keepdims=True), sink)
    p = jnp.exp(s - m)
    p = p / (jnp.sum(p, axis=-1, keepdims=True) + jnp.exp(sink - m))
    return jnp.einsum('...hgqk,...khd->...qhgd', p.astype(v.dtype), v)


def swa_prompt_context(k, v):
    B, T = k.shape[:2]
    nb = T // BLOCK
    def band(t):
        tr = jnp.concatenate([jnp.zeros_like(t[:, :BLOCK]), t], axis=1)
        tr = tr.reshape(B, nb + 1, BLOCK, SWA_KV_HEADS, HEAD_DIM)
        return jnp.concatenate([tr[:, :-1], tr[:, 1:]], axis=2)
    kb, vb = band(k), band(v)
    blk = jnp.arange(nb)[:, None] * BLOCK
    qpos = blk + jnp.arange(BLOCK)[None, :]
    kpos = blk - BLOCK + jnp.arange(2 * BLOCK)[None, :]
    rel = qpos[:, :, None] - kpos[:, None, :]
    mask = (rel >= 0) & (rel < WINDOW) & (kpos[:, None, :] >= 0)
    def attend(q, sinks):
        qb = q.reshape(B, nb, BLOCK, SWA_KV_HEADS, SWA_GROUP, HEAD_DIM)
        return sink_attention(qb, kb, vb, mask, sinks).reshape(B, T, MAIN_WIDTH)
    win = min(WINDOW, T)
    return attend, k[:, T - win:], v[:, T - win:]


def swa_sample_context(k, v, k_buf, v_buf):
    B, T = k.shape[:2]
    W = k_buf.shape[1]
    kc = jnp.concatenate([k_buf.astype(k.dtype), k], axis=1)
    vc = jnp.concatenate([v_buf.astype(v.dtype), v], axis=1)
    qpos = PAST_LEN + jnp.arange(T)
    kpos = PAST_LEN - W + jnp.arange(W + T)
    rel = qpos[:, None] - kpos[None, :]
    mask = (rel >= 0) & (rel < WINDOW)
    def attend(q, sinks):
        qh = q.reshape(B, T, SWA_KV_HEADS, SWA_GROUP, HEAD_DIM)
        return sink_attention(qh, kc, vc, mask, sinks).reshape(B, T, MAIN_WIDTH)
    return attend, kc[:, T:], vc[:, T:]


def trunk(x, pos, mem_k, mem_v, shift0, wkv0, make_swa, *, norm_mix, norm_mlp, w_out, w_up, w_down,
          mem_qnorm, w_in_a, shift_mu, w_w2, w0, w_a2, a0, w_g2, k_k, k_a, r_k, lnx_w, lnx_b,
          w_in_b, swa_qnorm, sinks, kv_norm, w_kv, swa_knorm):
    B, T, _ = x.shape
    h = x
    new_shift, new_wkv = [], []
    attend = k_state = v_state = None
    for l in range(DEPTH):
        hn = rms_norm(h, norm_mix[l])
        if l < N_A:
            proj = hn @ w_in_a[l]
            main, sh, st = rwkv_time_mix(proj[..., :RWKV_COLS], shift0[l], wkv0[l], shift_mu[l], w_w2[l],
                                         w0[l], w_a2[l], a0[l], w_g2[l], k_k[l], k_a[l], r_k[l],
                                         lnx_w[l], lnx_b[l])
            new_shift.append(sh)
            new_wkv.append(st)
            q_mem = proj[..., RWKV_COLS:]
        else:
            j = l - N_A
            proj = hn @ w_in_b[j]
            q = rms_norm(proj[..., :MAIN_WIDTH].reshape(B, T, SWA_Q_HEADS, HEAD_DIM), swa_qnorm[j])
            main = attend(rope(q, pos), sinks[j])
            q_mem = proj[..., MAIN_WIDTH:]
        mem_o = memory_attention(q_mem, mem_qnorm[l], mem_k[l], mem_v[l])
        h = h + jnp.concatenate([main, mem_o], axis=-1) @ w_out[l]
        hn = rms_norm(h, norm_mlp[l])
        h = h + jnp.square(jax.nn.relu(hn @ w_up[l])) @ w_down[l]
        if l == N_A - 1:
            kv = rms_norm(h, kv_norm) @ w_kv
            k_sh = rope(rms_norm(kv[..., :KV_WIDTH].reshape(B, T, SWA_KV_HEADS, HEAD_DIM), swa_knorm), pos)
            v_sh = kv[..., KV_WIDTH:].reshape(B, T, SWA_KV_HEADS, HEAD_DIM)
            attend, k_state, v_state = make_swa(k_sh, v_sh)
    return h, jnp.stack(new_shift), jnp.stack(new_wkv), k_state, v_state


def setup_inputs(seed: int = 0) -> dict:
    key = jax.random.key(seed)
    ks = iter(jax.random.split(key, 48))
    def nrm(shape, scale=1.0):
        return jax.random.normal(next(ks), shape, jnp.float32) * scale
    def gain(shape):
        return 1.0 + nrm(shape, 0.05)
    def unif(shape, lo, hi):
        return jax.random.uniform(next(ks), shape, jnp.float32, lo, hi)
    win = min(WINDOW, PAST_LEN)
    return {
        "x_prompt": nrm((BATCH, SEQ, D_MODEL)),
        "x_sample": nrm((DEC_BATCH, DEC_SEQ, D_MODEL)),
        "state_rwkv_shift": nrm((N_A, DEC_BATCH, RWKV_COLS)),
        "state_rwkv_wkv": nrm((N_A, DEC_BATCH, RWKV_HEADS, HEAD_DIM, HEAD_DIM), 0.5),
        "cache_swa_k": nrm((DEC_BATCH, win, SWA_KV_HEADS, HEAD_DIM)),
        "cache_swa_v": nrm((DEC_BATCH, win, SWA_KV_HEADS, HEAD_DIM)),
        "cache_mem_k": nrm((DEPTH, DEC_BATCH, N_MEM, MEM_HEADS, HEAD_DIM)),
        "cache_mem_v": nrm((DEPTH, DEC_BATCH, N_MEM, MEM_HEADS, HEAD_DIM)),
        "mem_prompt": nrm((BATCH, N_MEM, D_MODEL)),
        "norm_mix": gain((DEPTH, D_MODEL)),
        "norm_mlp": gain((DEPTH, D_MODEL)),
        "w_out": nrm((DEPTH, MIX_WIDTH, D_MODEL), MIX_WIDTH ** -0.5),
        "w_up": nrm((DEPTH, D_MODEL, D_FF), D_MODEL ** -0.5),
        "w_down": nrm((DEPTH, D_FF, D_MODEL), D_FF ** -0.5),
        "mem_norm": gain((DEPTH, D_MODEL)),
        "w_mem_kv": nrm((DEPTH, D_MODEL, 2 * MEM_WIDTH), D_MODEL ** -0.5),
        "mem_qnorm": gain((DEPTH, HEAD_DIM)),
        "mem_knorm": gain((DEPTH, HEAD_DIM)),
        "w_in_a": nrm((N_A, D_MODEL, A_IN_COLS), D_MODEL ** -0.5),
        "shift_mu": unif((N_A, RWKV_COLS), 0.0, 1.0),
        "w_w2": nrm((N_A, DECAY_LORA, MAIN_WIDTH), 0.1),
        "w0": unif((N_A, MAIN_WIDTH), -6.5, -1.5),
        "w_a2": nrm((N_A, AAA_LORA, MAIN_WIDTH), AAA_LORA ** -0.5),
        "a0": nrm((N_A, MAIN_WIDTH), 0.1),
        "w_g2": nrm((N_A, GATE_LORA, MAIN_WIDTH), GATE_LORA ** -0.5),
        "k_k": 0.85 + nrm((N_A, MAIN_WIDTH), 0.05),
        "k_a": gain((N_A, MAIN_WIDTH)),
        "r_k": nrm((N_A, RWKV_HEADS, HEAD_DIM), 0.1),
        "lnx_w": gain((N_A, MAIN_WIDTH)),
        "lnx_b": nrm((N_A, MAIN_WIDTH), 0.02),
        "w_in_b": nrm((N_B, D_MODEL, MIX_WIDTH), D_MODEL ** -0.5),
        "swa_qnorm": gain((N_B, HEAD_DIM)),
        "sinks": nrm((N_B, SWA_Q_HEADS)),
        "kv_norm": gain((D_MODEL,)),
        "w_kv": nrm((D_MODEL, 2 * KV_WIDTH), D_MODEL ** -0.5),
        "swa_knorm": gain((HEAD_DIM,)),
    }


def reference(x_prompt, x_sample, state_rwkv_shift, state_rwkv_wkv, cache_swa_k, cache_swa_v,
              cache_mem_k, cache_mem_v, mem_prompt,
              norm_mix, norm_mlp, w_out, w_up, w_down, mem_norm, w_mem_kv, mem_qnorm, mem_knorm,
              w_in_a, shift_mu, w_w2, w0, w_a2, a0, w_g2, k_k, k_a, r_k, lnx_w, lnx_b,
              w_in_b, swa_qnorm, sinks, kv_norm, w_kv, swa_knorm):
    run = functools.partial(
        trunk, norm_mix=norm_mix, norm_mlp=norm_mlp, w_out=w_out, w_up=w_up, w_down=w_down,
        mem_qnorm=mem_qnorm, w_in_a=w_in_a, shift_mu=shift_mu, w_w2=w_w2, w0=w0, w_a2=w_a2, a0=a0,
        w_g2=w_g2, k_k=k_k, k_a=k_a, r_k=r_k, lnx_w=lnx_w, lnx_b=lnx_b, w_in_b=w_in_b,
        swa_qnorm=swa_qnorm, sinks=sinks, kv_norm=kv_norm, w_kv=w_kv, swa_knorm=swa_knorm)

    B, T, _ = x_prompt.shape
    mem_kv_p = [memory_kv(mem_prompt, mem_norm[l], w_mem_kv[l], mem_knorm[l]) for l in range(DEPTH)]
    p_mem_k = jnp.stack([kv[0] for kv in mem_kv_p])
    p_mem_v = jnp.stack([kv[1] for kv in mem_kv_p])
    shift0 = jnp.zeros((N_A, B, RWKV_COLS), x_prompt.dtype)
    wkv0 = jnp.zeros((N_A, B, RWKV_HEADS, HEAD_DIM, HEAD_DIM), x_prompt.dtype)
    y_prompt, p_shift, p_wkv, p_swa_k, p_swa_v = run(
        x_prompt, jnp.arange(T), p_mem_k, p_mem_v, shift0, wkv0, swa_prompt_context)

    Ts = x_sample.shape[1]
    sample_ctx = functools.partial(swa_sample_context, k_buf=cache_swa_k, v_buf=cache_swa_v)
    y_sample, s_shift, s_wkv, s_swa_k, s_swa_v = run(
        x_sample, PAST_LEN + jnp.arange(Ts), cache_mem_k, cache_mem_v, state_rwkv_shift, state_rwkv_wkv,
        sample_ctx)

    return (y_prompt, y_sample, p_shift, p_wkv, p_swa_k, p_swa_v, p_mem_k, p_mem_v,
            s_shift, s_wkv, s_swa_k, s_swa_v)
```

```python
import os
import numpy as np
from contextlib import ExitStack
import concourse.bass as bass
import concourse.mybir as mybir
from concourse.bass_utils import run_bass_kernel_spmd

F32 = mybir.dt.float32
BF16 = mybir.dt.bfloat16
I32 = mybir.dt.int32
ALU = mybir.AluOpType
AF = mybir.ActivationFunctionType
AX = mybir.AxisListType

D = 1024
KC = 8
NG = 6
RW = 2560
AIN = 2816
DFF = 4096
PAST_LEN = 16384
ATTN_SCALE = 0.125
NB = 16
NTS = 64
ENGS = ("pe", "act", "dve", "pool", "sp")
NDSEM = 12


class Prog:
    def __init__(self, nc):
        self.nc = nc
        self.es = ExitStack()
        self.ops = {e: [] for e in ENGS}
        self.cnt = {e: 0 for e in ENGS}
        self.seen = {e: {} for e in ENGS}
        self.lastw = {}
        self.readers = {}
        self.ndma = 0
        self.dma_last = {}
        self.sems = {}
        self.pending = {e: False for e in ENGS}
        self.scopes = []

    def scope(self):
        es = ExitStack()
        self.scopes.append(es)
        return es

    def sb(self, name, shape, dt=F32, es=None):
        return (es or self.es).enter_context(self.nc.sbuf_tensor(name, list(shape), dt))

    def ps(self, name, shape, dt=F32, es=None):
        return (es or self.es).enter_context(self.nc.psum_tensor(name, list(shape), dt))

    def sem(self, key):
        if key not in self.sems:
            self.sems[key] = self.es.enter_context(self.nc.semaphore("s_" + str(key)))
        return self.sems[key]

    def _deps(self, reads, writes):
        toks = []
        for r in reads:
            if r in self.lastw:
                toks.append(self.lastw[r])
        for w in writes:
            if w in self.lastw:
                toks.append(self.lastw[w])
            toks.extend(self.readers.get(w, ()))
        return toks

    def _commit(self, tok, reads, writes):
        for r in reads:
            if r not in writes:
                self.readers.setdefault(r, []).append(tok)
        for w in writes:
            self.lastw[w] = tok
            self.readers[w] = []

    def _waits(self, eng, toks):
        need = {}
        for (sk, v) in toks:
            if eng == "pe" and sk == "pe":
                continue
            if v > need.get(sk, 0):
                need[sk] = v
        out = []
        for sk, v in need.items():
            if self.seen[eng].get(sk, 0) >= v:
                continue
            self.seen[eng][sk] = v
            out.append((sk, v))
        return out

    def op(self, eng, fn, reads=(), writes=(), inc=True):
        reads = tuple(reads)
        writes = tuple(writes)
        waits = self._waits(eng, self._deps(reads, writes))
        if inc:
            self.cnt[eng] += 1
            tok = (eng, self.cnt[eng])
            self.pending[eng] = False
        else:
            tok = (eng, self.cnt[eng] + 1)
            self.pending[eng] = True
        self.ops[eng].append((waits, fn, (eng, 1) if inc else None))
        self._commit(tok, reads, writes)
        return tok

    def dma(self, out, in_, reads=(), writes=(), **kw):
        reads = tuple(reads)
        writes = tuple(writes)
        i = self.ndma
        self.ndma += 1
        sk = "d%d" % (i % NDSEM)
        val = 16 * (i // NDSEM + 1)
        toks = self._deps(reads, writes)
        if i >= NDSEM:
            toks.append((sk, val - 16))
        waits = self._waits("sp", toks)
        self.ops["sp"].append((waits, (lambda e: e.dma_start(out=out, in_=in_, **kw)), (sk, 16)))
        self._commit((sk, val), reads, writes)
        self.dma_last[sk] = val
        return (sk, val)

    def barrier(self):
        toks = [(e, self.cnt[e]) for e in ENGS if e != "sp" and self.cnt[e] > 0]
        toks += list(self.dma_last.items())
        for e in ENGS:
            assert not self.pending[e]
            waits = self._waits(e, toks)
            if waits:
                self.ops[e].append((waits, None, None))

    def emit(self):
        nc = self.nc
        fin = [(sk, v) for sk, v in self.dma_last.items() if self.seen["sp"].get(sk, 0) < v]
        for e in ENGS:
            assert not self.pending[e], e
        for sk in list(self.dma_last.keys()) + [e for e in ENGS if e != "sp"]:
            self.sem(sk)
        sems = self.sems
        ops = self.ops

        final = [(e2, self.cnt[e2]) for e2 in ENGS if e2 != "sp" and self.cnt[e2] > 0] + list(self.dma_last.items())
        self.sem("fin")
        allsems = list(sems.values())

        def run(e, lst, ename):
            for (waits, fn, inc) in lst:
                for (sk, v) in waits:
                    e.wait_ge(sems[sk], v)
                if fn is None:
                    continue
                ins = fn(e)
                if inc is not None:
                    ins.then_inc(sems[inc[0]], inc[1])
            for (sk, v) in final:
                e.wait_ge(sems[sk], v)
            if ename != "pool":
                e.sem_inc(sems["fin"], 1)
            else:
                e.wait_ge(sems["fin"], 4)
                for sh in allsems:
                    e.sem_clear(sh)

        with nc.Block() as block:
            @block.sync
            def _(e):
                run(e, ops["sp"], "sp")

            @block.tensor
            def _(e):
                run(e, ops["pe"], "pe")

            @block.scalar
            def _(e):
                run(e, ops["act"], "act")

            @block.vector
            def _(e):
                run(e, ops["dve"], "dve")

            @block.gpsimd
            def _(e):
                run(e, ops["pool"], "pool")
        for es in reversed(self.scopes):
            es.close()
        self.es.close()

    def tt(self, eng, out, a, b, op, r, w):
        return self.op(eng, lambda e: e.tensor_tensor(out, a, b, op), r, w)

    def ts(self, eng, out, a, s1, s2, op0, op1, r, w):
        if s2 is None:
            return self.op(eng, lambda e: e.tensor_scalar(out, a, s1, None, op0), r, w)
        return self.op(eng, lambda e: e.tensor_scalar(out, a, s1, s2, op0, op1), r, w)

    def stt(self, eng, out, a, s, b, op0, op1, r, w):
        return self.op(eng, lambda e: e.scalar_tensor_tensor(out, a, s, b, op0, op1), r, w)

    def cp(self, eng, out, a, r, w):
        if eng == "act":
            return self.op(eng, lambda e: e.copy(out, a), r, w)
        return self.op(eng, lambda e: e.tensor_copy(out, a), r, w)

    def act(self, out, a, func, r, w, bias=None, scale=1.0, accum=None):
        kw = {}
        if bias is not None:
            kw["bias"] = bias
        if accum is not None:
            kw["accum_out"] = accum
        return self.op("act", lambda e: e.activation(out, a, func, scale=scale, **kw), r, w)

    def mm(self, out, lhsT, rhs, r, w, start=True, stop=True, inc=True):
        return self.op("pe", lambda e: e.matmul(out, lhsT, rhs, start=start, stop=stop), r, w, inc=inc)

    def tr(self, out, a, ident, r, w, inc=True):
        return self.op("pe", lambda e: e.transpose(out, a, ident), r, w, inc=inc)

    def memset(self, eng, ap, val, w):
        return self.op(eng, lambda e: e.memset(ap, val), (), w)

    def recip(self, out, a, r, w):
        return self.op("dve", lambda e: e.reciprocal(out, a), r, w)

    def reduce(self, out, a, r, w, op=ALU.add):
        return self.op("dve", lambda e: e.tensor_reduce(out, a, AX.X, op), r, w)


def bc(ap, shape):
    return ap.to_broadcast(list(shape))


class _Stop(Exception):
    pass


def build_program(SEGT, debug=False, stop=None):
    try:
        return _build_program(SEGT, debug, stop)
    except _Stop as e:
        return e.args[0]


def _build_program(SEGT, debug=False, stop=None):
    NMAIN = SEGT + 1
    NPRE = 3 * SEGT - 1
    NT = NPRE + NMAIN
    NTL = NMAIN + 1
    nc = bass.Bass("TRN2", target_bir_lowering=False)
    P = Prog(nc)

    def ckpt(name):
        if stop == name:
            if os.environ.get("BARAT"):
                for _e in ENGS:
                    P.pending[_e] = False
                P.barrier()
            raise _Stop((nc, P, dict(NT=NT, NPRE=NPRE, NMAIN=NMAIN, NTL=NTL)))

    def din(name, shape, dt=F32):
        return nc.dram_tensor(name, list(shape), dt, kind="ExternalInput").ap()

    def dout(name, shape, dt=F32):
        return nc.dram_tensor(name, list(shape), dt, kind="ExternalOutput").ap()

    xall = din("xall", [NT * 128, D])
    xs = din("xs", [NTS, D])
    sshift = din("sshift", [NB, RW])
    swkv = din("swkv", [NB, 12, 64, 64])
    sck = din("sck", [NB, 128, 256])
    scv = din("scv", [NB, 128, 256])
    smk = din("smk", [2, NB, 256, 256])
    smv = din("smv", [2, NB, 256, 256])
    memp = din("memp", [256, D])
    pos0 = din("pos0", [128, 1])
    hflag = din("hflag", [128, 1])
    norm_mix = din("norm_mix", [2, D])
    norm_mlp = din("norm_mlp", [2, D])
    w_out = din("w_out", [2, D, D])
    w_up = din("w_up", [2, D, DFF])
    w_down = din("w_down", [2, DFF, D])
    mem_norm = din("mem_norm", [2, D])
    w_mem_kv = din("w_mem_kv", [2, D, 512])
    mem_qnorm = din("mem_qnorm", [2, 64])
    mem_knorm = din("mem_knorm", [2, 64])
    w_in_a = din("w_in_a", [1, D, AIN])
    shift_mu = din("shift_mu", [1, RW])
    w_w2 = din("w_w2", [1, 64, 768])
    w0 = din("w0", [1, 768])
    w_a2 = din("w_a2", [1, 64, 768])
    a0 = din("a0", [1, 768])
    w_g2 = din("w_g2", [1, 128, 768])
    k_k = din("k_k", [1, 768])
    k_a = din("k_a", [1, 768])
    r_k = din("r_k", [1, 12, 64])
    lnx_w = din("lnx_w", [1, 768])
    lnx_b = din("lnx_b", [1, 768])
    w_in_b = din("w_in_b", [1, D, D])
    swa_qnorm = din("swa_qnorm", [1, 64])
    sinks = din("sinks", [1, 12])
    kv_norm = din("kv_norm", [D])
    w_kv = din("w_kv", [D, 512])
    swa_knorm = din("swa_knorm", [64])

    o_yp = dout("o_yp", [SEGT * 128, D])
    o_ys = dout("o_ys", [NTS, D])
    o_pshift = dout("o_pshift", [20, 128])
    o_pwkv = dout("o_pwkv", [12, 64, 64])
    o_pk = dout("o_pk", [128, 256])
    o_pv = dout("o_pv", [128, 256])
    o_pmk = dout("o_pmk", [2, 256, 256])
    o_pmv = dout("o_pmv", [2, 256, 256])
    o_sshift = dout("o_sshift", [NB, RW])
    o_swkv = dout("o_swkv", [NB, 12, 64, 64])
    o_sk = dout("o_sk", [NB, 128, 256])
    o_sv = dout("o_sv", [NB, 128, 256])
    hscr = nc.dram_tensor("hscr", [NTL * 128, D], F32, kind="ExternalOutput").ap()

    psA = P.ps("psA", [128, 512])
    psT = P.ps("psT", [128, 1024], BF16)
    psG = P.ps("psG", [128, 512])
    psW = P.ps("psW", [128, 512])
    psD = P.ps("psD", [128, 512])
    psP = P.ps("psP", [128, 512])
    psY0 = P.ps("psY0", [128, 512])
    psY1 = P.ps("psY1", [128, 512])

    ident_f = P.sb("ident_f", [128, 128])
    ident_b = P.sb("ident_b", [128, 128], BF16)
    identF64 = P.sb("identF64", [128, 64])
    bones = P.sb("bones", [128, 128], BF16)
    E2 = P.sb("E2", [128, 2], BF16)
    Rrot = P.sb("Rrot", [128, 128], BF16)
    MK128 = P.sb("MK128", [128, 128])
    ML = P.sb("ML", [128, 64])
    MC = P.sb("MC", [128, 128])
    MPm = P.sb("MPm", [128, 128])
    MPf = P.sb("MPf", [128, 128])
    MSC = P.sb("MSC", [128, 4])
    MN = P.sb("MN", [64, 4])
    SBm = P.sb("SBm", [64, 64])
    tmpi = P.sb("tmpi", [128, 128], I32)
    tmpf = P.sb("tmpf", [128, 128])
    c_eps6 = P.sb("c_eps6", [128, 1])
    c_eps24 = P.sb("c_eps24", [128, 1])
    c_lnx = P.sb("c_lnx", [128, 1])
    hfl = P.sb("hfl", [128, 1])
    p0t = P.sb("p0t", [128, 1])
    mu = P.sb("mu", [128, 20])
    w0f = P.sb("w0f", [128, 6])
    a0f = P.sb("a0f", [128, 6])
    kkf = P.sb("kkf", [128, 6])
    kaf = P.sb("kaf", [128, 6])
    omka = P.sb("omka", [128, 6])
    rkf = P.sb("rkf", [128, 6])
    lnw = P.sb("lnw", [128, 768])
    lnb = P.sb("lnb", [128, 768])
    gmq = P.sb("gmq", [128, 2])
    gmk = P.sb("gmk", [128, 2, 64])
    gsq = P.sb("gsq", [128, 1])
    gsk = P.sb("gsk", [128, 1])
    esink = P.sb("esink", [128, 12])
    gains = P.sb("gains", [128, 7, 8])
    frq = P.sb("frq", [128, 1])
    STGW = 1024
    stg = [P.sb("stg%d" % i, [128, STGW]) for i in range(3)]
    KTmz = [P.sb("KTmz%d" % e, [128, 2, 2, 256], BF16) for e in range(2)]
    V1m = P.sb("V1m", [128, 2, 2, 4, 66], BF16)
    stgi = [0]

    def sload(dst_ap, src_ap, name):
        P.dma(dst_ap, src_ap, writes=[name], allow_slow_non_contiguous=True)

    P.memset("pool", ident_f[:], 1.0, ["ident_f"])
    P.op("pool", lambda e: e.affine_select(ident_f[:], ident_f[:], [[1, 128]], ALU.is_equal, 0.0, base=0, channel_multiplier=-1), ["ident_f"], ["ident_f"])
    P.cp("pool", ident_b[:], ident_f[:], ["ident_f"], ["ident_b"])
    P.tt("pool", identF64[:], ident_f[:, 0:64], ident_f[:, 64:128], ALU.add, ["ident_f"], ["identF64"])
    P.memset("pool", bones[:], 0.0, ["bones"])
    P.memset("pool", bones[0:64, 0:64], 1.0, ["bones"])
    P.memset("pool", bones[64:128, 64:128], 1.0, ["bones"])
    P.memset("pool", E2[:], 0.0, ["E2"])
    P.memset("pool", E2[0:64, 0:1], 1.0, ["E2"])
    P.memset("pool", E2[64:128, 1:2], 1.0, ["E2"])
    P.memset("pool", tmpf[:], 1.0, ["tmpf"])
    P.op("pool", lambda e: e.affine_select(tmpf[:], tmpf[:], [[1, 128]], ALU.is_equal, 0.0, base=-32, channel_multiplier=-1), ["tmpf"], ["tmpf"])
    P.memset("pool", tmpf[:, 0:32], 0.0, ["tmpf"])
    P.memset("pool", tmpf[:, 64:96], 0.0, ["tmpf"])
    P.memset("pool", MC[:], 1.0, ["MC"])
    P.op("pool", lambda e: e.affine_select(MC[:], MC[:], [[1, 128]], ALU.is_equal, 0.0, base=32, channel_multiplier=-1), ["MC"], ["MC"])
    P.memset("pool", MC[:, 32:64], 0.0, ["MC"])
    P.memset("pool", MC[:, 96:128], 0.0, ["MC"])
    P.tt("pool", tmpf[:], tmpf[:], MC[:], ALU.subtract, ["tmpf", "MC"], ["tmpf"])
    P.cp("pool", Rrot[:], tmpf[:], ["tmpf"], ["Rrot"])
    P.op("pool", lambda e: e.iota(tmpi[:, 0:64], [[1, 64]], base=0, channel_multiplier=-1), (), ["tmpi"])
    P.cp("pool", tmpf[:, 0:64], tmpi[:, 0:64], ["tmpi", "tmpf"], ["tmpf"])
    P.ts("dve", tmpf[64:128, 0:64], tmpf[64:128, 0:64], 64.0, None, ALU.add, None, ["tmpf"], ["tmpf"])
    P.ts("dve", MK128[:, 0:64], tmpf[:, 0:64], 0.0, None, ALU.is_gt, None, ["tmpf"], ["MK128"])
    P.ts("dve", MK128[:, 64:128], tmpf[:, 0:64], 0.0, None, ALU.is_ge, None, ["tmpf"], ["MK128"])
    P.ts("dve", ML[:], tmpf[:, 0:64], 0.0, None, ALU.is_lt, None, ["tmpf"], ["ML"])
    P.memset("pool", MC[:], 1.0, ["MC"])
    P.op("pool", lambda e: e.affine_select(MC[:], MC[:], [[1, 128]], ALU.is_ge, 0.0, base=0, channel_multiplier=-1), ["MC"], ["MC"])
    P.memset("pool", MPm[:], 1.0, ["MPm"])
    P.op("pool", lambda e: e.affine_select(MPm[:], MPm[:], [[-1, 128]], ALU.is_gt, 0.0, base=0, channel_multiplier=1), ["MPm"], ["MPm"])
    P.dma(hfl[:], hflag[:, :], writes=["hfl"])
    P.dma(p0t[:], pos0[:, :], writes=["p0t"])
    P.ts("dve", MPf[:], MPm[:], hfl[:, 0:1], None, ALU.mult, None, ["MPm", "hfl"], ["MPf"])
    P.memset("pool", MSC[:], 1.0, ["MSC"])
    P.op("pool", lambda e: e.affine_select(MSC[:], MSC[:], [[-1, 4]], ALU.is_ge, 0.0, base=-1, channel_multiplier=1), ["MSC"], ["MSC"])
    P.memset("pool", MN[:], 1.0, ["MN"])
    P.op("pool", lambda e: e.affine_select(MN[:], MN[:], [[16, 4]], ALU.is_ge, 0.0, base=15, channel_multiplier=-1), ["MN"], ["MN"])
    P.op("pool", lambda e: e.iota(tmpi[0:64, 0:64], [[-1, 64]], base=64, channel_multiplier=1), ["tmpi"], ["tmpi"])
    P.cp("dve", tmpf[0:64, 0:64], tmpi[0:64, 0:64], ["tmpi", "tmpf"], ["tmpf"])
    P.ts("dve", tmpf[0:64, 0:64], tmpf[0:64, 0:64], 0.0625, None, ALU.mult, None, ["tmpf"], ["tmpf"])
    P.cp("dve", tmpi[0:64, 0:64], tmpf[0:64, 0:64], ["tmpf", "tmpi"], ["tmpi"])
    P.cp("dve", SBm[:], tmpi[0:64, 0:64], ["tmpi"], ["SBm"])
    P.tt("dve", SBm[:], SBm[:], tmpf[0:64, 0:64], ALU.is_equal, ["SBm", "tmpf"], ["SBm"])
    P.memset("pool", c_eps6[:], 1e-6, ["c_eps6"])
    P.memset("pool", c_eps24[:], 1e-24, ["c_eps24"])
    P.memset("pool", c_lnx[:], 6.4e-4, ["c_lnx"])
    P.op("pool", lambda e: e.iota(tmpi[:, 64:65], [[0, 1]], base=0, channel_multiplier=1), ["tmpi"], ["tmpi"])
    P.cp("dve", frq[:], tmpi[:, 64:65], ["tmpi"], ["frq"])
    P.ts("dve", tmpf[:, 64:65], frq[:], 1.0 / 32, -0.484375, ALU.mult, ALU.add, ["frq", "tmpf"], ["tmpf"])
    P.cp("dve", tmpi[:, 65:66], tmpf[:, 64:65], ["tmpf", "tmpi"], ["tmpi"])
    P.cp("dve", tmpf[:, 64:65], tmpi[:, 65:66], ["tmpi", "tmpf"], ["tmpf"])
    P.stt("dve", frq[:], tmpf[:, 64:65], -32.0, frq[:], ALU.mult, ALU.add, ["tmpf", "frq"], ["frq"])
    P.act(frq[:], frq[:], AF.Exp, ["frq"], ["frq"], scale=float(-np.log(10000.0) / 32.0))

    sload(mu[:], shift_mu[0].rearrange("(g p) -> p g", p=128), "mu")
    sload(w0f[:], w0[0].rearrange("(g p) -> p g", p=128), "w0f")
    sload(a0f[:], a0[0].rearrange("(g p) -> p g", p=128), "a0f")
    sload(kkf[:], k_k[0].rearrange("(g p) -> p g", p=128), "kkf")
    sload(kaf[:], k_a[0].rearrange("(g p) -> p g", p=128), "kaf")
    sload(rkf[:], r_k[0].rearrange("(g e) j -> (e j) g", e=2), "rkf")
    P.ts("dve", omka[:], kaf[:], -1.0, 1.0, ALU.mult, ALU.add, ["kaf"], ["omka"])
    P.dma(lnw[:], lnx_w[0:1, :].partition_broadcast(128), writes=["lnw"])
    P.dma(lnb[:], lnx_b[0:1, :].partition_broadcast(128), writes=["lnb"])
    for l in range(2):
        sload(gmq[0:64, l:l + 1], mem_qnorm[l:l + 1, :].rearrange("o j -> j o"), "gmq")
        sload(gmq[64:128, l:l + 1], mem_qnorm[l:l + 1, :].rearrange("o j -> j o"), "gmq")
        P.dma(gmk[:, l, :], mem_knorm[l:l + 1, :].partition_broadcast(128), writes=["gmk"])
    P.ts("dve", gmq[:], gmq[:], ATTN_SCALE, None, ALU.mult, None, ["gmq"], ["gmq"])
    sload(gsq[0:64, :], swa_qnorm[0:1, :].rearrange("o j -> j o"), "gsq")
    sload(gsq[64:128, :], swa_qnorm[0:1, :].rearrange("o j -> j o"), "gsq")
    P.ts("dve", gsq[:], gsq[:], ATTN_SCALE, None, ALU.mult, None, ["gsq"], ["gsq"])
    sload(gsk[0:64, :], swa_knorm.rearrange("(j o) -> j o", o=1), "gsk")
    sload(gsk[64:128, :], swa_knorm.rearrange("(j o) -> j o", o=1), "gsk")
    P.dma(esink[:], sinks[0:1, :].partition_broadcast(128), writes=["esink"])
    P.act(esink[:], esink[:], AF.Exp, ["esink"], ["esink"])
    for i, src in enumerate([norm_mix[0], norm_mix[1], norm_mlp[0], norm_mlp[1], kv_norm, mem_norm[0], mem_norm[1]]):
        sload(gains[:, i, :], src.rearrange("(k p) -> p k", p=128), "gains")

    ckpt("A")
    castrot = [0]

    def load_w(src2d, K, N, dst, gain=None, dst_name=None, col_map=None, row0=0):
        nk = K // 128
        for k in range(nk):
            for n0 in range(0, N, STGW):
                n1 = min(N, n0 + STGW)
                s = stg[stgi[0] % 3]
                sn = "stg%d" % (stgi[0] % 3)
                stgi[0] += 1
                P.dma(s[:, 0:n1 - n0], src2d[k * 128:(k + 1) * 128, n0:n1], writes=[sn])
                o = dst[:, k, n0:n1]
                if gain is not None:
                    P.ts("dve", o, s[:, 0:n1 - n0], gain[:, k:k + 1], None, ALU.mult, None, [sn, "gains"], [dst_name])
                else:
                    eng = "act" if castrot[0] % 2 == 0 else "pool"
                    castrot[0] += 1
                    P.cp(eng, o, s[:, 0:n1 - n0], [sn], [dst_name])

    def rms_rows(x_ap, xname, out_ap, oname, nrow, junk, ss, rs, jname="junk"):
        P.memset("pool", ss[0:nrow, :], 0.0, ["ss"])
        P.act(junk[0:nrow, :], x_ap, AF.Square, [xname, "ss"], [jname, "ss"], accum=ss[0:nrow, :])
        P.act(rs[0:nrow, :], ss[0:nrow, :], AF.Sqrt, ["ss", "c_eps6"], ["rs"], bias=c_eps6[0:nrow, :], scale=1.0 / D)
        P.recip(rs[0:nrow, :], rs[0:nrow, :], ["rs"], ["rs"])
        P.ts("dve", out_ap, x_ap, rs[0:nrow, 0:1], None, ALU.mult, None, [xname, "rs"], [oname])

    def to_fm(src_bf, sname, nrow, nchunk, dst, dname):
        for c in range(nchunk):
            P.tr(psT[:, c * 128:c * 128 + nrow], src_bf[0:nrow, c * 128:(c + 1) * 128], ident_b[0:nrow, 0:nrow], [sname, "ident_b"], ["psT"], inc=(c == nchunk - 1))
        P.cp("act", dst[:, 0:nchunk, 0:nrow], psT[:, 0:nchunk * 128].rearrange("p (c t) -> p c t", c=nchunk)[:, :, 0:nrow], ["psT"], [dname])

    esA = P.scope()
    Wmk = P.sb("Wmk", [128, KC, 512], BF16, esA)
    memx = P.sb("memx", [128, 2, D], F32, esA)
    memh = P.sb("memh", [128, D], BF16, esA)
    memT = P.sb("memT", [128, KC, 256], BF16, esA)
    junkA = P.sb("junkA", [128, D], BF16, esA)
    ssA = P.sb("ssA", [128, 1], F32, esA)
    rsA = P.sb("rsA", [128, 1], F32, esA)
    kvt = P.sb("kvt", [128, 512], F32, esA)
    ksq = P.sb("ksq", [128, 256], F32, esA)
    kss = P.sb("kss", [128, 4], F32, esA)
    knb = P.sb("knb", [128, 256], BF16, esA)
    for mc in range(2):
        P.dma(memx[:, mc, :], memp[mc * 128:(mc + 1) * 128, :], writes=["memx"])
        rms_rows(memx[:, mc, :], "memx", memh[:], "memh", 128, junkA, ssA, rsA)
        for c in range(KC):
            P.tr(psT[:, c * 128:(c + 1) * 128], memh[:, c * 128:(c + 1) * 128], ident_b[:], ["memh", "ident_b"], ["psT"], inc=(c == KC - 1))
        P.cp("act", memT[:, :, mc * 128:(mc + 1) * 128], psT[:].rearrange("p (c t) -> p c t", c=KC), ["psT"], ["memT"])
    P.memset("pool", V1m[:], 1.0, ["V1m"])
    for e in range(2):
        P.memset("pool", KTmz[e][:], 0.0, ["KTmz%d" % e])
    for l in range(2):
        load_w(w_mem_kv[l], D, 512, Wmk, gain=gains[:, 5 + l, :], dst_name="Wmk")
        for mc in range(2):
            for k in range(KC):
                P.mm(psA[:], memT[:, k, mc * 128:(mc + 1) * 128], Wmk[:, k, :], ["memT", "Wmk"], ["psA"], start=(k == 0), stop=(k == KC - 1), inc=(k == KC - 1))
            P.cp("act", kvt[:], psA[:], ["psA"], ["kvt"])
            P.dma(o_pmv[l, mc * 128:(mc + 1) * 128, :], kvt[:, 256:512], reads=["kvt"])
            P.cp("pool", V1m[:, l, mc, :, 0:64], kvt[:, 256:512].rearrange("p (h d) -> p h d", h=4), ["kvt"], ["V1m"])
            P.tt("dve", ksq[:], kvt[:, 0:256], kvt[:, 0:256], ALU.mult, ["kvt"], ["ksq"])
            P.reduce(kss[:], ksq[:].rearrange("p (h d) -> p h d", h=4), ["ksq"], ["kss"])
            P.act(kss[:], kss[:], AF.Sqrt, ["kss", "c_eps6"], ["kss"], bias=c_eps6[:], scale=1.0 / 64)
            P.recip(kss[:], kss[:], ["kss"], ["kss"])
            P.tt("dve", ksq[:].rearrange("p (h d) -> p h d", h=4), kvt[:, 0:256].rearrange("p (h d) -> p h d", h=4), bc(kss[:].unsqueeze(2), [128, 4, 64]), ALU.mult, ["kvt", "kss"], ["ksq"])
            P.tt("dve", ksq[:].rearrange("p (h d) -> p h d", h=4), ksq[:].rearrange("p (h d) -> p h d", h=4), bc(gmk[:, l, :].unsqueeze(1), [128, 4, 64]), ALU.mult, ["ksq", "gmk"], ["ksq"])
            P.dma(o_pmk[l, mc * 128:(mc + 1) * 128, :], ksq[:], reads=["ksq"])
            P.cp("act", knb[:], ksq[:], ["ksq"], ["knb"])
            for hp in range(2):
                P.tr(psT[:, hp * 128:(hp + 1) * 128], knb[:, hp * 128:(hp + 1) * 128], ident_b[:], ["knb", "ident_b"], ["psT"], inc=(hp == 1))
            for e in range(2):
                hs = slice(e * 64, (e + 1) * 64)
                P.cp("act", KTmz[e][hs, l, :, mc * 128:(mc + 1) * 128], psT[hs, 0:256].rearrange("p (h t) -> p h t", h=2), ["psT"], ["KTmz%d" % e])
    P.barrier()
    esA.close()
    ckpt("A2")

    esB = P.scope()
    Wina = P.sb("Wina", [128, KC, AIN], BF16, esB)
    WL = P.sb("WL", [128, 3, 768], BF16, esB)
    Wout = P.sb("Wout", [128, KC, D], BF16, esB)
    load_w(w_in_a[0], D, AIN, Wina, gain=gains[:, 0, :], dst_name="Wina")
    load_w(w_out[0], D, D, Wout, dst_name="Wout")
    P.memset("pool", WL[:, 0:2, :], 0.0, ["WL"])
    P.dma(stg[0][0:64, 0:768], w_w2[0], writes=["stg0"])
    P.cp("act", WL[0:64, 0, :], stg[0][0:64, 0:768], ["stg0"], ["WL"])
    P.dma(stg[1][64:128, 0:768], w_a2[0], writes=["stg1"])
    P.cp("act", WL[64:128, 1, :], stg[1][64:128, 0:768], ["stg1"], ["WL"])
    P.dma(stg[2][:, 0:768], w_g2[0], writes=["stg2"])
    P.cp("act", WL[:, 2, :], stg[2][:, 0:768], ["stg2"], ["WL"])

    ckpt("Bw")
    esB1 = P.scope()
    X = [P.sb("X%d" % i, [128, D], F32, esB) for i in range(2)]
    xh = P.sb("xh", [128, D], BF16, esB)
    ss = P.sb("ss", [128, 1], F32, esB)
    rs = P.sb("rs", [128, 1], F32, esB)
    hnT = P.sb("hnT", [128, KC, 128], BF16, esB)
    Pall = P.sb("Pall", [128, 20, 129], F32, esB)
    PS = P.sb("PS", [128, 20, 128], F32, esB)
    QM = P.sb("QM", [128, 2, 128], F32, esB)
    T = [P.sb("T%d" % i, [128, 6, 128], F32, esB) for i in range(7)]
    WAb = P.sb("WAb", [128, 128], BF16, esB)
    sg = P.sb("sg", [128, 128], BF16, esB)
    sqb = P.sb("sqb", [128, 6, 128], BF16, esB)
    VB = P.sb("VB", [128, 6, 128], BF16, esB)
    rkp = P.sb("rkp", [128, 6, 128], BF16, esB)
    VT = P.sb("VT", [128, 6, 128], BF16, esB)
    cat = P.sb("cat", [128, D], BF16, esB)
    catT = P.sb("catT", [128, KC, 128], BF16, esB)
    QN = P.sb("QN", [128, 2, 128], BF16, esB)
    PTm = P.sb("PTm", [128, 8, 128], BF16, esB)
    gst = P.sb("gst", [128, 24], F32, esB)
    gst2 = P.sb("gst2", [128, 24], F32, esB)
    mor = P.sb("mor", [128, 4], F32, esB)
    MTbz = [P.sb("MTbz%d" % e, [128, 2, 2, 64], BF16, esB1) for e in range(2)]
    QTbz = [P.sb("QTbz%d" % e, [128, 2, 2, 64], BF16, esB1) for e in range(2)]
    AR = P.sb("AR", [128, 6, 2, 128], BF16, esB1)
    AFM = P.sb("AFM", [128, 6, 128], BF16, esB1)
    KH = P.sb("KH", [128, 6, 128], BF16, esB1)
    BH = P.sb("BH", [128, 6, 128], BF16, esB1)
    ATm = P.sb("ATm", [128, 6, 128], BF16, esB1)
    BHT = P.sb("BHT", [128, 6, 128], BF16, esB1)
    HS = P.sb("HS", [128, 6, 64], F32, esB1)
    HSb = [P.sb("HSb%d" % i, [128, 6, 64], BF16, esB1) for i in range(2)]
    GSz = [[P.sb("GS%dz%d" % (i, c), [128, 4, 128], BF16, esB1) for c in range(2)] for i in range(2)]
    PPz = [[P.sb("PP%dz%d" % (i, c), [128, 2, 4, 64], BF16, esB1) for c in range(2)] for i in range(2)]
    KBz = [P.sb("KBz%d" % e, [128, 6, 2, 128], BF16, esB1) for e in range(2)]
    KHTz = [P.sb("KHTz%d" % c, [128, 6, 128], BF16, esB1) for c in range(2)]
    BHTz = [P.sb("BHTz%d" % c, [128, 6, 128], BF16, esB1) for c in range(2)]
    WFz = [P.sb("WFz%d" % c, [128, 4, 64], BF16, esB1) for c in range(2)]
    Wf = P.sb("Wf", [128, 4, 128], F32, esB1)
    Wb = [P.sb("Wb%d" % i, [128, 4, 128], BF16, esB1) for i in range(2)]
    DG = P.sb("DG", [128, 2, 2, 64], F32, esB1)
    junk = xh

    for i in range(2):
        for c in range(2):
            P.memset("pool", GSz[i][c][:], 0.0, ["GS%dz%d" % (i, c)])
            P.memset("pool", PPz[i][c][:], 0.0, ["PP%dz%d" % (i, c)])
        P.memset("pool", KBz[i][:], 0.0, ["KBz%d" % i])
        P.memset("pool", KHTz[i][:], 0.0, ["KHTz%d" % i])
        P.memset("pool", BHTz[i][:], 0.0, ["BHTz%d" % i])
        P.memset("pool", WFz[i][:], 0.0, ["WFz%d" % i])
        P.memset("pool", MTbz[i][:], 0.0, ["MTbz%d" % i])
        P.memset("pool", QTbz[i][:], 0.0, ["QTbz%d" % i])
    P.memset("pool", Pall[:], 0.0, ["Pall"])
    P.memset("pool", HS[:], 0.0, ["HS"])
    P.memset("pool", HSb[0][:], 0.0, ["HSb0_0", "HSb0_1", "HSb0_2"])

    def v3(t, n):
        return t[:, :, 0:n]

    def inproj(chunks, ntok, Pdst, pname, col0):
        i = 0
        while i < len(chunks):
            grp = [chunks[i]]
            while len(grp) < 4 and i + len(grp) < len(chunks) and chunks[i + len(grp)] == grp[-1] + 1 and (chunks[i + len(grp)] < 20) == (grp[0] < 20):
                grp.append(chunks[i + len(grp)])
            i += len(grp)
            for j, cc in enumerate(grp):
                for k in range(KC):
                    P.mm(psA[:, j * 128:j * 128 + ntok], Wina[:, k, cc * 128:(cc + 1) * 128], hnT[:, k, 0:ntok], ["Wina", "hnT"], ["psA"],
                         start=(k == 0), stop=(k == KC - 1), inc=(k == KC - 1 and j == len(grp) - 1))
            src = psA[:, 0:len(grp) * 128].rearrange("p (c t) -> p c t", c=len(grp))[:, :, 0:ntok]
            if grp[0] < 20:
                P.cp("act", Pdst[:, grp[0]:grp[0] + len(grp), col0:col0 + ntok], src, ["psA"], [pname])
            else:
                P.cp("act", QM[:, 0:len(grp), 0:ntok], src, ["psA"], ["QM"])

    def rwkv_prep(ntok, main, PSv):
        n = ntok
        P.act(WAb[0:64, 0:n], PSv(18, 19)[0:64, 0, :], AF.Tanh, ["PS"], ["WAb"])
        P.cp("act", WAb[64:128, 0:n], PSv(18, 19)[64:128, 0, :], ["PS"], ["WAb"])
        for g in range(NG):
            P.mm(psG[:, (g % 4) * 128:(g % 4) * 128 + n] if g < 4 else psW[:, (g - 4) * 128:(g - 4) * 128 + n],
                 WL[:, 0, g * 128:(g + 1) * 128], WAb[:, 0:n], ["WL", "WAb"], ["psG"] if g < 4 else ["psW"], inc=(g in (3, 5)))
        w0b4 = bc(w0f[:, 0:4].unsqueeze(2), [128, 4, n])
        w0b2 = bc(w0f[:, 4:6].unsqueeze(2), [128, 2, n])
        P.tt("dve", T[0][:, 0:4, 0:n], psG[:].rearrange("p (c t) -> p c t", c=4)[:, :, 0:n], w0b4, ALU.add, ["psG", "w0f"], ["T0"])
        P.tt("dve", T[0][:, 4:6, 0:n], psW[:, 0:256].rearrange("p (c t) -> p c t", c=2)[:, :, 0:n], w0b2, ALU.add, ["psW", "w0f"], ["T0"])
        P.act(v3(T[0], n), v3(T[0], n), AF.Exp, ["T0"], ["T0"], scale=-1.0)
        P.ts("dve", v3(T[0], n), v3(T[0], n), 1.0, -1.0 / 0.6065306597126334, ALU.add, ALU.mult, ["T0"], ["T0"])
        P.recip(v3(T[0], n), v3(T[0], n), ["T0"], ["T0"])
        for g in range(NG):
            P.mm(psG[:, (g % 4) * 128:(g % 4) * 128 + n] if g < 4 else psW[:, (g - 4) * 128:(g - 4) * 128 + n],
                 WL[:, 1, g * 128:(g + 1) * 128], WAb[:, 0:n], ["WL", "WAb"], ["psG"] if g < 4 else ["psW"], inc=(g in (3, 5)))
        a0b4 = bc(a0f[:, 0:4].unsqueeze(2), [128, 4, n])
        a0b2 = bc(a0f[:, 4:6].unsqueeze(2), [128, 2, n])
        P.tt("dve", T[1][:, 0:4, 0:n], psG[:].rearrange("p (c t) -> p c t", c=4)[:, :, 0:n], a0b4, ALU.add, ["psG", "a0f"], ["T1"])
        P.tt("dve", T[1][:, 4:6, 0:n], psW[:, 0:256].rearrange("p (c t) -> p c t", c=2)[:, :, 0:n], a0b2, ALU.add, ["psW", "a0f"], ["T1"])
        P.act(v3(T[1], n), v3(T[1], n), AF.Sigmoid, ["T1"], ["T1"])
        kv_ = PSv(6, 12)
        P.tt("dve", v3(T[6], n), kv_, bc(kkf[:].unsqueeze(2), [128, 6, n]), ALU.mult, ["PS", "kkf"], ["T6"])
        P.act(v3(sqb, n), v3(T[6], n), AF.Square, ["T6"], ["sqb"])
        for g in range(NG):
            P.mm(psG[:, (g % 4) * 128:(g % 4) * 128 + n] if g < 4 else psW[:, (g - 4) * 128:(g - 4) * 128 + n],
                 bones[:], sqb[:, g, 0:n], ["bones", "sqb"], ["psG"] if g < 4 else ["psW"], inc=(g in (3, 5)))
        P.act(T[5][:, 0:4, 0:n], psG[:].rearrange("p (c t) -> p c t", c=4)[:, :, 0:n], AF.Sqrt, ["psG", "c_eps24"], ["T5"], bias=c_eps24[:])
        P.act(T[5][:, 4:6, 0:n], psW[:, 0:256].rearrange("p (c t) -> p c t", c=2)[:, :, 0:n], AF.Sqrt, ["psW", "c_eps24"], ["T5"], bias=c_eps24[:])
        P.recip(v3(T[5], n), v3(T[5], n), ["T5"], ["T5"])
        P.tt("dve", v3(T[6], n), v3(T[6], n), v3(T[5], n), ALU.mult, ["T6", "T5"], ["T6"])
        P.tt("pool", v3(T[2], n), v3(T[6], n), v3(T[1], n), ALU.mult, ["T6", "T1"], ["T2"])
        P.tt("dve", v3(T[3], n), v3(T[1], n), bc(kaf[:].unsqueeze(2), [128, 6, n]), ALU.mult, ["T1", "kaf"], ["T3"])
        P.tt("dve", v3(T[3], n), v3(T[3], n), bc(omka[:].unsqueeze(2), [128, 6, n]), ALU.add, ["T3", "omka"], ["T3"])
        P.tt("dve", v3(T[3], n), v3(T[3], n), kv_, ALU.mult, ["T3", "PS"], ["T3"])
        if main:
            P.act(sg[:, 0:n], PSv(19, 20)[:, 0, :], AF.Sigmoid, ["PS"], ["sg"])
            P.tt("pool", v3(T[5], n), PSv(0, 6), v3(T[3], n), ALU.mult, ["PS", "T3"], ["T5"])
            P.tt("dve", v3(rkp, n), v3(T[5], n), bc(rkf[:].unsqueeze(2), [128, 6, n]), ALU.mult, ["T5", "rkf"], ["rkp"])

    rstm = P.sb("rstm", [128, 768], F32, esB1)
    P.memset("pool", rstm[:], 1.0, ["rstm"])
    P.memset("pool", rstm[:].rearrange("p (c t) -> p c t", t=64)[:, :, 0:1], 0.0, ["rstm"])

    def post_and_out(ntok, ysrc_fn, vtm_ap, qm_fn, l, Xt, xname, mem_fn):
        n = ntok
        Ty = T[0][:].rearrange("p g t -> p (g t)")
        Tq = T[1][:].rearrange("p g t -> p (g t)")
        Tz = T[2][:].rearrange("p g t -> p (g t)")
        ysrc_fn(Ty)
        y3 = Ty[0:n, :].rearrange("p (h i) -> p h i", h=12)
        q3 = Tq[0:n, :].rearrange("p (h i) -> p h i", h=12)
        P.reduce(gst[0:n, 0:12], y3, ["T0"], ["gst"])
        P.tt("pool", q3, y3, y3, ALU.mult, ["T0"], ["T1"])
        P.reduce(gst[0:n, 12:24], q3, ["T1"], ["gst"])
        P.ts("dve", gst[0:n, :], gst[0:n, :], 1.0 / 64, None, ALU.mult, None, ["gst"], ["gst"])
        P.tt("dve", gst2[0:n, 0:12], gst[0:n, 0:12], gst[0:n, 0:12], ALU.mult, ["gst"], ["gst2"])
        P.tt("dve", gst2[0:n, 0:12], gst[0:n, 12:24], gst2[0:n, 0:12], ALU.subtract, ["gst", "gst2"], ["gst2"])
        P.act(gst2[0:n, 0:12], gst2[0:n, 0:12], AF.Sqrt, ["gst2", "c_lnx"], ["gst2"], bias=c_lnx[0:n, :])
        P.recip(gst2[0:n, 0:12], gst2[0:n, 0:12], ["gst2"], ["gst2"])
        P.tt("dve", y3, y3, bc(gst[0:n, 0:12].unsqueeze(2), [n, 12, 64]), ALU.subtract, ["T0", "gst"], ["T0"])
        P.tt("dve", y3, y3, bc(gst2[0:n, 0:12].unsqueeze(2), [n, 12, 64]), ALU.mult, ["T0", "gst2"], ["T0"])
        P.tt("pool", Ty[0:n, :], Ty[0:n, :], lnw[0:n, :], ALU.mult, ["T0", "lnw"], ["T0"])
        P.tt("pool", Ty[0:n, :], Ty[0:n, :], lnb[0:n, :], ALU.add, ["T0", "lnb"], ["T0"])
        for g in range(NG):
            P.mm(psD[0:n, 2 * g:2 * g + 2], rkp[:, g, 0:n], E2[:], ["rkp", "E2"], ["psD"], inc=(g == NG - 1))
        P.cp("act", gst[0:n, 0:12], psD[0:n, 0:12], ["psD"], ["gst"])
        P.tt("dve", q3, vtm_ap, bc(gst[0:n, 0:12].unsqueeze(2), [n, 12, 64]), ALU.mult, ["VT", "gst"], ["T1"])
        P.tt("dve", Ty[0:n, :], Ty[0:n, :], Tq[0:n, :], ALU.add, ["T0", "T1"], ["T0"])
        P.mm(psA[0:n, :], sg[:, 0:n], WL[:, 2, 0:512], ["sg", "WL"], ["psA"])
        P.mm(psW[0:n, 0:256], sg[:, 0:n], WL[:, 2, 512:768], ["sg", "WL"], ["psW"])
        P.tt("dve", cat[0:n, 0:512], Ty[0:n, 0:512], psA[0:n, :], ALU.mult, ["T0", "psA"], ["cat"])
        P.tt("dve", cat[0:n, 512:768], Ty[0:n, 512:768], psW[0:n, 0:256], ALU.mult, ["T0", "psW"], ["cat"])
        mem_fn(l, n, qm_fn)
        finish_mixer(n, Xt, xname)

    def qnorm_fm(n, src3, sname, nchunk, gcol, dst3, dname):
        Tq = T[4]
        P.act(sqb[:, 0:nchunk, 0:n], src3, AF.Square, [sname], ["sqb"])
        for c in range(nchunk):
            P.mm(psD[:, c * 128:c * 128 + n], bones[:], sqb[:, c, 0:n], ["bones", "sqb"], ["psD"], inc=(c == nchunk - 1))
        P.act(Tq[:, 0:nchunk, 0:n], psD[:, 0:nchunk * 128].rearrange("p (c t) -> p c t", c=nchunk)[:, :, 0:n], AF.Sqrt, ["psD", "c_eps6"], ["T4"], bias=c_eps6[:], scale=1.0 / 64)
        P.recip(Tq[:, 0:nchunk, 0:n], Tq[:, 0:nchunk, 0:n], ["T4"], ["T4"])
        P.stt("dve", dst3, src3, gcol, Tq[:, 0:nchunk, 0:n], ALU.mult, ALU.mult, [sname, "T4", "gmq", "gsq", "gsk"], [dname])

    def mem_attn_prompt(l, n, qm_fn):
        qm3 = qm_fn()
        qnorm_fm(n, qm3, "QM", 2, gmq[:, l:l + 1], QN[:, :, 0:n], "QN")
        for h in range(4):
            e, gp = h % 2, h // 2
            for mc in range(2):
                dstp = psG if h < 2 else psP
                col = ((h % 2) * 2 + mc) * 128
                P.mm(dstp[:, col:col + n], KTmz[e][:, l, gp, mc * 128:(mc + 1) * 128], QN[:, gp, 0:n],
                     ["KTmz%d" % e, "QN"], ["psG"] if h < 2 else ["psP"], inc=(mc == 1 and h % 2 == 1))
        P.act(PTm[:, 0:4, 0:n], psG[:].rearrange("p (c t) -> p c t", c=4)[:, :, 0:n], AF.Exp, ["psG"], ["PTm"])
        P.act(PTm[:, 4:8, 0:n], psP[:].rearrange("p (c t) -> p c t", c=4)[:, :, 0:n], AF.Exp, ["psP"], ["PTm"])
        for h in range(4):
            for mc in range(2):
                P.mm(psD[0:n, h * 66:h * 66 + 65], PTm[:, h * 2 + mc, 0:n], V1m[:, l, mc, h, 0:65], ["PTm", "V1m"], ["psD"],
                     start=(mc == 0), stop=(mc == 1), inc=(h == 3 and mc == 1))
        pd = psD[0:n, 0:264].rearrange("p (h c) -> p h c", h=4)
        P.recip(mor[0:n, :], pd[:, :, 64], ["psD"], ["mor"])
        P.tt("dve", cat[0:n, 768:1024].rearrange("p (h d) -> p h d", h=4), pd[:, :, 0:64], bc(mor[0:n, :].unsqueeze(2), [n, 4, 64]), ALU.mult, ["psD", "mor"], ["cat"])

    def finish_mixer(n, Xt, xname):
        to_fm(cat, "cat", n, KC, catT, "catT")
        for half in range(2):
            dstp = psA if half == 0 else psG
            for k in range(KC):
                P.mm(dstp[0:n, :], catT[:, k, 0:n], Wout[:, k, half * 512:(half + 1) * 512], ["catT", "Wout"], ["psA"] if half == 0 else ["psG"],
                     start=(k == 0), stop=(k == KC - 1), inc=(k == KC - 1))
            P.tt("dve", Xt[0:n, half * 512:(half + 1) * 512], Xt[0:n, half * 512:(half + 1) * 512], dstp[0:n, :], ALU.add, [xname, "psA" if half == 0 else "psG"], [xname])

    for ti in range(NT):
        main = ti >= NPRE
        Xt = X[ti % 2]
        xname = "X%d" % (ti % 2)
        P.dma(Xt[:], xall[ti * 128:(ti + 1) * 128, :], writes=[xname])
        rms_rows(Xt[:], xname, xh[:], "xh", 128, junk, ss, rs, jname="xh")
        to_fm(xh, "xh", 128, KC, hnT, "hnT")
        fullp = ti >= NPRE - 1
        chunks = list(range(22)) if main else (list(range(20)) if fullp else list(range(6, 19)))
        inproj(chunks, 128, Pall, "Pall", 1)
        c0, c1 = (0, 20) if fullp else (6, 19)
        P.tt("pool", PS[:, c0:c1, :], Pall[:, c0:c1, 0:128], Pall[:, c0:c1, 1:129], ALU.subtract, ["Pall"], ["PS"])
        P.tt("dve", PS[:, c0:c1, :], PS[:, c0:c1, :], bc(mu[:, c0:c1].unsqueeze(2), [128, c1 - c0, 128]), ALU.mult, ["PS", "mu"], ["PS"])
        P.tt("pool", PS[:, c0:c1, :], PS[:, c0:c1, :], Pall[:, c0:c1, 1:129], ALU.add, ["PS", "Pall"], ["PS"])
        if ti == NT - 1:
            P.cp("pool", tmpf[:, 0:20], Pall[:, :, 128], ["Pall", "tmpf"], ["tmpf"])
            P.op("pe", lambda e: e.transpose(psD[0:20, 0:128], tmpf[:, 0:20], ident_f[:]), ["tmpf", "ident_f"], ["psD"])
            P.cp("act", tmpf[0:20, :], psD[0:20, 0:128], ["psD", "tmpf"], ["tmpf"])
            P.dma(o_pshift[:, :], tmpf[0:20, :], reads=["tmpf"])
        P.cp("act", Pall[:, :, 0:1], Pall[:, :, 128:129], ["Pall"], ["Pall"])
        if ti == 0:
            ckpt("t0proj")
        rwkv_prep(128, main, lambda a, b: PS[:, a:b, :])
        if ti == 0:
            ckpt("t0prep")
        ldf = T[0][:].rearrange("p g t -> p (g t)")
        cumf = T[4][:].rearrange("p g t -> p (g t)")
        P.op("dve", lambda e: e.tensor_tensor_scan(cumf, rstm[:], ldf, 0.0, ALU.mult, ALU.add), ["rstm", "T0"], ["T4"])
        if ti == 0:
            ckpt("t0scan")
        P.tt("pool", T[5][:], T[4][:], T[0][:], ALU.subtract, ["T4", "T0"], ["T5"])
        P.act(T[5][:], T[5][:], AF.Exp, ["T5"], ["T5"])
        P.act(T[0][:], T[4][:], AF.Exp, ["T4"], ["T0"])
        P.act(T[4][:], T[4][:], AF.Exp, ["T4"], ["T4"], scale=-1.0)
        if ti == 0:
            ckpt("t0exp")
        gC = T[0][:].rearrange("p g (c t) -> p g c t", c=2)[:, :, :, 63:64]
        P.stt("dve", AR[:, :, :, 0:64], T[6][:].rearrange("p g (c t) -> p g c t", c=2), -1.0, T[5][:].rearrange("p g (c t) -> p g c t", c=2), ALU.mult, ALU.mult, ["T6", "T5"], ["AR"])
        P.stt("dve", AFM[:], T[6][:], -1.0, T[5][:], ALU.mult, ALU.mult, ["T6", "T5"], ["AFM"])
        P.tt("dve", T[2][:], T[2][:], T[4][:], ALU.mult, ["T2", "T4"], ["T2"])
        for e in range(2):
            hs = slice(e * 64, (e + 1) * 64)
            P.cp("act" if e == 0 else "pool", KBz[e][hs, :, :, 64:128], T[2][hs].rearrange("p g (c t) -> p g c t", c=2), ["T2"], ["KBz%d" % e])
        P.tt("dve", BH[:].rearrange("p g (c t) -> p g c t", c=2), T[2][:].rearrange("p g (c t) -> p g c t", c=2), bc(gC, [128, 6, 2, 64]), ALU.mult, ["T2", "T0"], ["BH"])
        P.tt("dve", T[6][:], T[3][:], T[4][:], ALU.mult, ["T3", "T4"], ["T6"])
        for e in range(2):
            hs = slice(e * 64, (e + 1) * 64)
            P.cp("act" if e == 0 else "pool", KBz[e][hs, :, :, 0:64], T[6][hs].rearrange("p g (c t) -> p g c t", c=2), ["T6"], ["KBz%d" % e])
        P.tt("dve", KH[:].rearrange("p g (c t) -> p g c t", c=2), T[6][:].rearrange("p g (c t) -> p g c t", c=2), bc(gC, [128, 6, 2, 64]), ALU.mult, ["T6", "T0"], ["KH"])
        if main:
            P.tt("dve", AR[:, :, :, 64:128], PS[:, 0:6, :].rearrange("p g (c t) -> p g c t", c=2), T[0][:].rearrange("p g (c t) -> p g c t", c=2), ALU.mult, ["PS", "T0"], ["AR"])
        P.cp("act", VB[:], PS[:, 12:18, :], ["PS"], ["VB"])
        if ti == 0:
            ckpt("t0arkb")
        for (src, sname, dst, dname) in ((AFM, "AFM", ATm, "ATm"), (KH, "KH", None, "KHT"), (BH, "BH", BHT, "BHT"), (VB, "VB", VT, "VT")):
            for g in range(NG):
                in_ap = src[:, g, :]
                P.tr(psT[:, g * 128:(g + 1) * 128], in_ap, ident_b[:], [sname, "ident_b"], ["psT"], inc=(g == NG - 1))
            if dst is not None:
                P.cp("act", dst[:].rearrange("p g t -> p (g t)"), psT[:, 0:768], ["psT"], [dname])
            if dname in ("KHT", "BHT"):
                zb = KHTz if dname == "KHT" else BHTz
                for c in range(2):
                    hs = slice(c * 64, (c + 1) * 64)
                    P.cp("act", zb[c][hs].rearrange("p g t -> p (g t)"), psT[hs, 0:768], ["psT"], ["%sz%d" % (dname, c)])

        if ti == 0:
            ckpt("t0tm")
        NW = 128 if main else 64
        _os2 = os
        for bt in range(3):
            hA, hB = "HSb%d_%d" % (0, bt), "HSb%d_%d" % (1, bt)
            heads = [(2 * bt + gg, e) for gg in range(2) for e in range(2)]
            for slot in range(2):
                for hh, (g, e) in enumerate(heads):
                    for c in range(2):
                        P.mm(psG[c * 64:(c + 1) * 64, hh * 128:hh * 128 + NW], KBz[e][:, g, c, slot * 64:(slot + 1) * 64], AR[:, g, c, 0:NW],
                             ["KBz%d" % e, "AR"], ["psG"], inc=(hh == 3 and c == 1))
                for c in range(2):
                    hs = slice(c * 64, (c + 1) * 64)
                    P.tt("dve", GSz[slot][c][hs, :, 0:NW], psG[hs, :].rearrange("p (h t) -> p h t", h=4)[:, :, 0:NW], bc(MK128[hs, 0:NW].unsqueeze(1), [64, 4, NW]),
                         ALU.mult, ["psG", "MK128"], ["GS%dz%d" % (slot, c)])
            for hh, (g, e) in enumerate(heads):
                for c in range(2):
                    P.mm(psW[c * 64:(c + 1) * 64, hh * 64:(hh + 1) * 64], AR[:, g, c, 0:64], KBz[e][:, g, c, 64:128], ["AR", "KBz%d" % e], ["psW"], inc=False)
            for hh, (g, e) in enumerate(heads):
                for c in range(2):
                    P.mm(psW[c * 64:(c + 1) * 64, 256 + hh * 64:256 + (hh + 1) * 64], GSz[0][c][:, hh, 0:64], VT[:, g, e * 64:(e + 1) * 64], ["GS0z%d" % c, "VT"], ["psW"], inc=(hh == 3 and c == 1))
            for c in range(2):
                hs = slice(c * 64, (c + 1) * 64)
                P.tt("dve", PPz[0][c][hs, 0, :, :], psW[hs, 0:256].rearrange("p (h t) -> p h t", h=4), bc(ML[hs, :].unsqueeze(1), [64, 4, 64]), ALU.mult, ["psW", "ML"], ["PP0z%d" % c])
            P.cp("act", Wb[0][:, :, 64:128], psW[:, 256:512].rearrange("p (h t) -> p h t", h=4), ["psW"], ["Wb0"])
            P.cp("pool", Wb[0][:, :, 0:64], ATm[:, 2 * bt:2 * bt + 2, :].rearrange("p g (e j) -> p (g e) j", e=2), ["ATm"], ["Wb0"])
            P.cp("act", Wf[:, :, 64:128], psW[:, 256:512].rearrange("p (h t) -> p h t", h=4), ["psW"], ["Wf"])
            P.cp("pool", Wf[:, :, 0:64], ATm[:, 2 * bt:2 * bt + 2, :].rearrange("p g (e j) -> p (g e) j", e=2), ["ATm"], ["Wf"])
            if ti == 0 and bt == 0:
                ckpt("u_l")
            for lev in range(6):
                wcur, wnxt = Wb[lev % 2], Wb[(lev + 1) % 2]
                wcn, wnn = "Wb%d" % (lev % 2), "Wb%d" % ((lev + 1) % 2)
                if lev == 0:
                    Pk = lambda hh, c: PPz[0][c][:, 0, hh, :]
                    PTk = lambda hh, c: GSz[1][c][:, hh, 0:64]
                    pkn = lambda c: "PP0z%d" % c
                    ptn = lambda c: "GS1z%d" % c
                else:
                    Pk = lambda hh, c, q=lev % 2: PPz[q][c][:, 0, hh, :]
                    PTk = lambda hh, c, q=lev % 2: PPz[q][c][:, 1, hh, :]
                    pkn = lambda c, q=lev % 2: "PP%dz%d" % (q, c)
                    ptn = pkn
                for hh in range(4):
                    for c in range(2):
                        P.mm(psD[c * 64:(c + 1) * 64, hh * 128:(hh + 1) * 128], PTk(hh, c), wcur[:, hh, :], [ptn(c), wcn], ["psD"], inc=(hh == 3 and c == 1))
                P.tt("dve", wnxt[:], Wf[:], psD[:].rearrange("p (h t) -> p h t", h=4), ALU.add, ["Wf", "psD"], [wnn])
                P.tt("dve", Wf[:], Wf[:], psD[:].rearrange("p (h t) -> p h t", h=4), ALU.add, ["Wf", "psD"], ["Wf"])
                if lev < 5:
                    for hh in range(4):
                        for c in range(2):
                            P.mm(psP[c * 64:(c + 1) * 64, hh * 64:(hh + 1) * 64], PTk(hh, c), Pk(hh, c), [pkn(c), ptn(c)], ["psP"], inc=False)
                            P.mm(psP[c * 64:(c + 1) * 64, 256 + hh * 64:256 + (hh + 1) * 64], Pk(hh, c), PTk(hh, c), [pkn(c), ptn(c)], ["psP"], inc=(hh == 3 and c == 1))
                    for c in range(2):
                        hs = slice(c * 64, (c + 1) * 64)
                        P.cp("act" if c == 0 else "dve", PPz[(lev + 1) % 2][c][hs].rearrange("p s h t -> p (s h t)"), psP[hs, :], ["psP"], ["PP%dz%d" % ((lev + 1) % 2, c)])
            if ti == 0 and bt == 0:
                ckpt("u_lev")
            WF = Wb[0]
            wfn = "Wb0"
            for c in range(2):
                hs = slice(c * 64, (c + 1) * 64)
                P.cp("pool", WFz[c][hs, :, :], Wf[hs, :, 0:64], ["Wf"], ["WFz%d" % c])
            for hh, (g, e) in enumerate(heads):
                gg = g - 2 * bt
                for c in range(2):
                    col = (gg * 2 + c) * 64
                    P.mm(psW[e * 64:(e + 1) * 64, col:col + 64], WFz[c][:, hh, :], BHT[:, g, e * 64:(e + 1) * 64], ["WFz%d" % c, "BHT"], ["psW"], inc=(True if _os2.environ.get("INCALL") else (not main and hh == 3 and c == 1)))
            if main:
                for hh, (g, e) in enumerate(heads):
                    gg = g - 2 * bt
                    for c in range(2):
                        col = 256 + (gg * 2 + c) * 64
                        P.mm(psW[e * 64:(e + 1) * 64, col:col + 64], WFz[c][:, hh, :], GSz[1][c][:, hh, 64:128], ["WFz%d" % c, "GS1z%d" % c], ["psW"], inc=(hh == 3 and c == 1))
            gCb = T[0][:, 2 * bt:2 * bt + 2, :].rearrange("p g (c t) -> p g c t", c=2)[:, :, :, 63:64]
            P.tt("dve", DG[:], bc(identF64[:].unsqueeze(1).unsqueeze(1), [128, 2, 2, 64]), bc(gCb, [128, 2, 2, 64]), ALU.mult, ["identF64", "T0"], ["DG"])
            for e in range(2):
                hs = slice(e * 64, (e + 1) * 64)
                P.tt("dve", MTbz[e][hs], psW[hs, 0:256].rearrange("p (g c t) -> p g c t", g=2, c=2), DG[hs], ALU.add, ["psW", "DG"], ["MTbz%d" % e])
                if main:
                    P.tt("dve", QTbz[e][hs], psW[hs, 256:512].rearrange("p (g c t) -> p g c t", g=2, c=2), AR[hs, 2 * bt:2 * bt + 2, :, 64:128], ALU.add, ["psW", "AR"], ["QTbz%d" % e])
            if ti == 0 and bt == 0 and debug and stop == "u_mt":
                dbg = T[1][:].rearrange("p g t -> p (g t)")
                P.cp("dve", dbg[:, 0:256], MTbz[0][:].rearrange("p a b c -> p (a b c)"), ["MTbz0"], ["T1"])
                P.cp("dve", dbg[:, 256:512], MTbz[1][:].rearrange("p a b c -> p (a b c)"), ["MTbz1"], ["T1"])
                P.cp("dve", dbg[:, 512:768], DG[:].rearrange("p a b c -> p (a b c)"), ["DG"], ["T1"])
                P.dma(hscr[0:128, 0:768], dbg, reads=["T1"])
                dbg2 = T[2][:].rearrange("p g t -> p (g t)")
                P.cp("dve", dbg2[:, 0:256], WFz[0][:].rearrange("p a b -> p (a b)"), ["WFz0"], ["T2"])
                P.cp("dve", dbg2[:, 256:512], WFz[1][:].rearrange("p a b -> p (a b)"), ["WFz1"], ["T2"])
                P.dma(hscr[128:256, 0:512], dbg2[:, 0:512], reads=["T2"])
            if ti == 0 and bt == 0:
                ckpt("u_mt")
            if _os2.environ.get("BAR"):
                P.pending["pe"] = False
                P.barrier()
            for c in range(2):
                hcur, hnxt = HSb[c % 2], HSb[(c + 1) % 2]
                hcn, hnn = (hA, hB) if c == 0 else (hB, hA)
                if main:
                    for hh, (g, e) in enumerate(heads):
                        gg = g - 2 * bt
                        head = 2 * g + e
                        yp, yn = (psY0, "psY0") if head < 8 else (psY1, "psY1")
                        yo = yp[c * 64:(c + 1) * 64, (head % 8) * 64:(head % 8 + 1) * 64]
                        P.mm(yo, QTbz[e][:, gg, c, :], hcur[:, g, :], ["QTbz%d" % e, hcn], [yn], start=True, stop=False, inc=False)
                        P.mm(yo, GSz[0][c][:, hh, 64:128], VT[:, g, e * 64:(e + 1) * 64], ["GS0z%d" % c, "VT"], [yn], start=False, stop=False, inc=False)
                        P.mm(yo, GSz[1][c][:, hh, 64:128], WF[:, hh, 64:128], ["GS1z%d" % c, wfn], [yn], start=False, stop=True, inc=(hh == 3))
                for hh, (g, e) in enumerate(heads):
                    gg = g - 2 * bt
                    co = psP[e * 64:(e + 1) * 64, gg * 64:(gg + 1) * 64]
                    _os = os
                    CH = _os.environ.get("CH") or "123"
                    if CH == "1r":
                        P.mm(co, VT[:, g, 0:64], MTbz[e][:, gg, c, :], ["MTbz%d" % e, "VT"], ["psP"], start=True, stop=True, inc=(hh == 3))
                    elif CH == "1w":
                        P.cp("act", WFz[0][0:64, 0:1, :], MTbz[e][0:64, gg, c:c + 1, :], ["MTbz%d" % e, "WFz0"], ["WFz0"])
                        P.mm(co, WFz[0][:, 0, :], VT[:, g, 0:64], ["WFz0", "VT"], ["psP"], start=True, stop=True, inc=(hh == 3))
                    elif CH == "1v":
                        P.mm(co, MTbz[e][:, gg, c, :], VT[:, g, 0:64], ["MTbz%d" % e, "VT"], ["psP"], start=True, stop=True, inc=(hh == 3))
                    elif CH == "1k":
                        P.mm(co, KHTz[c][:, g, e * 64:(e + 1) * 64], hcur[:, g, :], ["KHTz%d" % c, hcn], ["psP"], start=True, stop=True, inc=(hh == 3))
                    elif "1" in CH:
                        P.mm(co, MTbz[e][:, gg, c, :], hcur[:, g, :], ["MTbz%d" % e, hcn], ["psP"], start=True, stop=(CH == "1"), inc=(CH == "1" and hh == 3))
                    if "2" in CH:
                        P.mm(co, KHTz[c][:, g, e * 64:(e + 1) * 64], VT[:, g, e * 64:(e + 1) * 64], ["KHTz%d" % c, "VT"], ["psP"], start=("1" not in CH), stop=("3" not in CH), inc=("3" not in CH and hh == 3))
                    if "3" in CH:
                        P.mm(co, BHTz[c][:, g, e * 64:(e + 1) * 64], WF[:, hh, 64:128], ["BHTz%d" % c, wfn], ["psP"], start=(CH == "3"), stop=True, inc=(hh == 3))
                if "a" not in _os.environ.get("EV", ""):
                    P.cp("act", HS[:, 2 * bt:2 * bt + 2, :], psP[:, 0:128].rearrange("p (g i) -> p g i", g=2), ["psP"], ["HS"])
                if "d" not in _os.environ.get("EV", ""):
                    P.cp("dve", hnxt[:, 2 * bt:2 * bt + 2, :], HS[:, 2 * bt:2 * bt + 2, :], ["HS"], [hnn])
                if ti == 0 and bt == 0 and c == 0:
                    ckpt("u_c0")
            if ti == 0 and bt == 0:
                ckpt("u_b0")
            if ti == 0 and bt == 1:
                ckpt("u_b1")
        if ti == 0:
            ckpt("t0units")
        if ti == NPRE - 1:
            ckpt("pre")
        if main:
            def ysrc(Ty):
                P.cp("act", Ty[:, 0:512], psY0[:], ["psY0"], ["T0"])
                P.cp("act", Ty[:, 512:768], psY1[:, 0:256], ["psY1"], ["T0"])
            post_and_out(128, ysrc, VT[:].rearrange("p g (e i) -> p (g e) i", e=2), lambda: QM[:, :, :], 0, Xt, xname, mem_attn_prompt)
            P.dma(hscr[(ti - NPRE) * 128:(ti - NPRE + 1) * 128, :], Xt[:], reads=[xname])
    Sout = T[1]
    for g in range(NG):
        P.op("pe", lambda e, g=g: e.transpose(psA[0:64, 0:128], HS[:, g, :], ident_f[:]), ["HS", "ident_f"], ["psA"])
        P.cp("act", Sout[0:64, g, :], psA[0:64, 0:128], ["psA"], ["T1"])
    P.dma(o_pwkv.rearrange("(g e) i j -> i g e j", e=2), Sout[0:64, :, :].rearrange("p g (e j) -> p g e j", e=2), reads=["T1"])

    ckpt("B")
    P.barrier()
    esB1.close()

    esB2 = P.scope()
    OH = P.sb("OH", [64, 64, 64], BF16, esB2)
    Sst = P.sb("Sst", [128, NB, 6, 64], F32, esB2)
    TMv = P.sb("TMv", [64, 6, 2, 384], BF16, esB2)
    Ys = T[4][:, :, 0:64]
    tS = T[5][:, :, 0:64]
    saS = P.sb("saS", [128, 6], F32, esB2)
    nbf = P.sb("nbf", [128, 6, 64], BF16, esB2)
    qtm = P.sb("qtm", [64, 256], BF16, esB2)
    Kc = stg[2][:, 0:512].rearrange("p (m c) -> p m c", m=2)
    Vc = stg[1][:, 0:512].rearrange("p (m c) -> p m c", m=2)
    Vc1 = P.sb("Vc1", [128, 2, 4, 66], BF16, esB2)
    sTm = P.sb("sTm", [128, 2, 4, 4], F32, esB2)
    LPm = PTm[:].rearrange("p a t -> p (a t)")[:, 0:512].rearrange("p (m h t) -> p m h t", m=2, h=4)
    accm = P.sb("accm", [64, 4, 66], F32, esB2)
    P.cp("dve", OH[:], bc(ident_f[0:64, 0:64].unsqueeze(2), [64, 64, 64]), ["ident_f"], ["OH"])
    P.memset("pool", Vc1[:], 1.0, ["Vc1"])
    P.memset("pool", LPm, 0.0, ["PTm"])
    for e in range(2):
        P.dma(Sst[e * 64:(e + 1) * 64, :, :, :], swkv.rearrange("b (g e) i j -> e i b g j", e=2)[e], writes=["Sst"])
    Xs, xsn = X[0], "X0"
    P.memset("pool", Xs[:], 0.0, [xsn])
    P.dma(Xs[0:64, :], xs[:, :], writes=[xsn])
    rms_rows(Xs[0:64, :], xsn, xh[0:64, :], "xh", 64, junk, ss, rs, jname="xh")
    to_fm(xh, "xh", 64, KC, hnT, "hnT")
    inproj(list(range(22)), 64, Pall, "Pall", 16)
    for j in range(5):
        st, sn = stg[j % 3], "stg%d" % (j % 3)
        P.dma(st[0:16, 0:512], sshift[:, j * 512:(j + 1) * 512], writes=[sn])
        for q in range(4):
            P.op("pe", lambda e, q=q, st=st: e.transpose(psA[:, q * 16:(q + 1) * 16], st[0:16, q * 128:(q + 1) * 128], ident_f[0:16, 0:16]), [sn, "ident_f"], ["psA"])
        P.cp("act", Pall[:, 4 * j:4 * j + 4, 0:16], psA[:, 0:64].rearrange("p (c t) -> p c t", c=4), ["psA"], ["Pall"])
    P.tt("pool", PS[:, 0:20, 0:64], Pall[:, :, 0:64], Pall[:, :, 16:80], ALU.subtract, ["Pall"], ["PS"])
    P.tt("dve", PS[:, 0:20, 0:64], PS[:, 0:20, 0:64], bc(mu[:, 0:20].unsqueeze(2), [128, 20, 64]), ALU.mult, ["PS", "mu"], ["PS"])
    P.tt("pool", PS[:, 0:20, 0:64], PS[:, 0:20, 0:64], Pall[:, :, 16:80], ALU.add, ["PS", "Pall"], ["PS"])
    for j in range(5):
        st, sn = stg[j % 3], "stg%d" % (j % 3)
        for q in range(4):
            P.op("pe", lambda e, q=q, j=j: e.transpose(psA[0:16, q * 128:(q + 1) * 128], Pall[:, 4 * j + q, 64:80], ident_f[:]), ["Pall", "ident_f"], ["psA"])
        P.cp("act", st[0:16, 0:512], psA[0:16, :], ["psA", sn], [sn])
        P.dma(o_sshift[:, j * 512:(j + 1) * 512], st[0:16, 0:512], reads=[sn])
    rwkv_prep(64, True, lambda a, b: PS[:, a:b, 0:64])
    P.act(T[4][:, :, 0:64], T[0][:, :, 0:64], AF.Exp, ["T0"], ["T4"])
    P.cp("act", nbf[:], T[4][:, :, 0:64], ["T4"], ["nbf"])
    P.cp("dve", tS, nbf[:], ["nbf"], ["T5"])
    P.tt("dve", T[5][:, :, 0:64], T[4][:, :, 0:64], tS, ALU.subtract, ["T4", "T5"], ["T5"])
    P.ts("dve", T[6][:, :, 0:64], T[6][:, :, 0:64], -1.0, None, ALU.mult, None, ["T6"], ["T6"])

    def to_rows(v, src3, sname, first=False):
        if not first:
            P.cp("act", nbf[:], src3, [sname], ["nbf"])
        for g in range(NG):
            P.tr(psT[0:64, g * 128:(g + 1) * 128], nbf[:, g, :], ident_b[:], ["nbf", "ident_b"], ["psT"], inc=(g == NG - 1))
        P.cp("act", TMv[:, v].rearrange("p e (g j) -> p e g j", g=6), psT[0:64, 0:768].rearrange("p (g e j) -> p e g j", g=6, e=2), ["psT"], ["TMv"])

    to_rows(0, None, None, first=True)
    to_rows(1, T[5][:, :, 0:64], "T5")
    to_rows(2, T[6][:, :, 0:64], "T6")
    to_rows(3, T[2][:, :, 0:64], "T2")
    to_rows(4, T[3][:, :, 0:64], "T3")
    to_rows(5, PS[:, 0:6, 0:64], "PS")
    bcp = ((2, psG, "psG"), (3, psW, "psW"), (4, psD, "psD"), (5, psP, "psP"))
    v6 = lambda pz: pz[:, 0:384].rearrange("p (g j) -> p g j", g=6)
    for t in range(4):
        for b in range(NB):
            k0 = t * 16 + b
            for e in range(2):
                hs = slice(e * 64, (e + 1) * 64)
                P.mm(psA[hs, 0:384], OH[:, k0, :], TMv[:, 0, e, :], ["OH", "TMv"], ["psA"], start=True, stop=False, inc=False)
                P.mm(psA[hs, 0:384], OH[:, k0, :], TMv[:, 1, e, :], ["OH", "TMv"], ["psA"], start=False, stop=True, inc=(e == 1))
            for (v, pz, pzn) in bcp:
                for e in range(2):
                    P.mm(pz[e * 64:(e + 1) * 64, 0:384], OH[:, k0, :], TMv[:, v, e, :], ["OH", "TMv"], [pzn], inc=(e == 1))
            Sb = Sst[:, b]
            P.tt("dve", tS, Sb, v6(psG), ALU.mult, ["Sst", "psG"], ["T5"])
            P.reduce(saS[:], tS, ["T5"], ["saS"])
            P.tt("dve", Sb, Sb, v6(psA), ALU.mult, ["Sst", "psA"], ["Sst"])
            P.tt("dve", tS, v6(psW), bc(saS[:].unsqueeze(2), [128, 6, 64]), ALU.mult, ["psW", "saS"], ["T5"])
            P.tt("dve", Sb, Sb, tS, ALU.add, ["Sst", "T5"], ["Sst"])
            P.tt("dve", tS, v6(psD), bc(PS[:, 12:18, k0:k0 + 1], [128, 6, 64]), ALU.mult, ["psD", "PS"], ["T5"])
            P.tt("dve", Sb, Sb, tS, ALU.add, ["Sst", "T5"], ["Sst"])
            P.tt("dve", tS, Sb, v6(psP), ALU.mult, ["Sst", "psP"], ["T5"])
            P.reduce(Ys[:, :, k0], tS, ["T5"], ["T4"])
    for e in range(2):
        P.dma(o_swkv.rearrange("b (g e) i j -> e i b g j", e=2)[e], Sst[e * 64:(e + 1) * 64, :, :, :], reads=["Sst"])
    P.cp("act", VB[:, :, 0:64], PS[:, 12:18, 0:64], ["PS"], ["VB"])
    for g in range(NG):
        P.tr(psT[0:64, g * 128:(g + 1) * 128], VB[:, g, 0:64], ident_b[:], ["VB", "ident_b"], ["psT"], inc=(g == NG - 1))
    P.cp("act", VT[0:64].rearrange("p g t -> p (g t)"), psT[0:64, 0:768], ["psT"], ["VT"])

    def ysrc_s(Ty):
        for g in range(NG):
            pz, pzn = (psY0, "psY0") if g < 4 else (psY1, "psY1")
            P.op("pe", lambda e, g=g, pz=pz: e.transpose(pz[0:64, (g % 4) * 128:(g % 4 + 1) * 128], Ys[:, g, :], ident_f[:]), ["T4", "ident_f"], [pzn])
        P.cp("act", Ty[0:64, 0:512], psY0[0:64, :], ["psY0"], ["T0"])
        P.cp("act", Ty[0:64, 512:768], psY1[0:64, 0:256], ["psY1"], ["T0"])

    def mem_attn_sample_impl(l, QNb, qnname, catb, catname):
        for gp in range(2):
            P.tr(psT[0:64, gp * 128:(gp + 1) * 128], QNb[:, gp, 0:64], ident_b[:], [qnname, "ident_b"], ["psT"], inc=(gp == 1))
        P.cp("act", qtm[:], psT[0:64, 0:256], ["psT"], ["qtm"])
        P.memset("pool", accm[:], 0.0, ["accm"])
        prod = X[1][:, 0:512]
        for b in range(NB):
            P.dma(Kc, smk[l, b].rearrange("(mc p) c -> p mc c", p=128), writes=["stg2"])
            P.dma(Vc, smv[l, b].rearrange("(mc p) c -> p mc c", p=128), writes=["stg1"])
            P.cp("pool", Vc1[:, :, :, 0:64], Vc.rearrange("p m (h d) -> p m h d", h=4), ["stg1"], ["Vc1"])
            for t in range(4):
                k0 = t * 16 + b
                pz, pzn = (psY0, "psY0") if t < 2 else (psY1, "psY1")
                for e in range(2):
                    P.mm(pz[e * 64:(e + 1) * 64, (t % 2) * 256:(t % 2 + 1) * 256], OH[:, k0, :], qtm[:, :], ["OH", "qtm"], [pzn], inc=(e == 1 and t % 2 == 1))
            for mc in range(2):
                for tp in range(2):
                    pz, pzn = (psY0, "psY0") if tp == 0 else (psY1, "psY1")
                    P.tt("dve", prod.rearrange("p (t c) -> p t c", t=2), pz[:, :].rearrange("p (t c) -> p t c", t=2), bc(Kc[:, mc, :].unsqueeze(1), [128, 2, 256]), ALU.mult, [pzn, "stg2"], ["X1"])
                    P.reduce(sTm[:, mc, tp * 2:(tp + 1) * 2, :], prod.rearrange("p (t h d) -> p t h d", t=2, h=4), ["X1"], ["sTm"])
            P.act(LPm[:, :, :, b:64:16], sTm[:].rearrange("p m t h -> p m h t"), AF.Exp, ["sTm"], ["PTm"])
            for h in range(4):
                for mc in range(2):
                    P.mm(psD[0:64, h * 66:h * 66 + 65], LPm[:, mc, h, :], Vc1[:, mc, h, 0:65], ["PTm", "Vc1"], ["psD"], start=(mc == 0), stop=(mc == 1), inc=(h == 3 and mc == 1))
            P.tt("dve", accm[:], accm[:], psD[0:64, 0:264].rearrange("p (h c) -> p h c", h=4), ALU.add, ["accm", "psD"], ["accm"])
            P.memset("pool", LPm[:, :, :, b:64:16], 0.0, ["PTm"])
        P.recip(mor[0:64, :], accm[:, :, 64], ["accm"], ["mor"])
        P.tt("dve", catb[0:64, 768:1024].rearrange("p (h d) -> p h d", h=4), accm[:, :, 0:64], bc(mor[0:64, :].unsqueeze(2), [64, 4, 64]), ALU.mult, ["accm", "mor"], [catname])

    def mem_attn_sample(l, n, qm_fn):
        qnorm_fm(64, qm_fn(), "QM", 2, gmq[:, l:l + 1], QN[:, :, 0:64], "QN")
        mem_attn_sample_impl(l, QN, "QN", cat, "cat")

    post_and_out(64, ysrc_s, VT[0:64].rearrange("p g (e i) -> p (g e) i", e=2), lambda: QM[:, :, 0:64], 0, Xs, xsn, mem_attn_sample)
    P.dma(hscr[NMAIN * 128:(NMAIN + 1) * 128, :], Xs[:], reads=[xsn])
    ckpt("B2")
    P.barrier()
    esB2.close()
    esB.close()

    esC = P.scope()
    hres = P.sb("hres", [128, NTL, D], F32, esC)
    xh2 = P.sb("xh2", [128, D], BF16, esC)
    ss2 = P.sb("ss_2", [128, 1], F32, esC)
    rs2 = P.sb("rs_2", [128, 1], F32, esC)
    for t in range(NTL):
        P.dma(hres[:, t, :], hscr[t * 128:(t + 1) * 128, :], reads=[], writes=["hres%d" % t])

    def mlp_phase(l, tiles):
        es = P.scope()
        nt = len(tiles)
        hnA = P.sb("hnA%d" % l, [128, KC, nt * 128], BF16, es)
        Wup = [P.sb("Wup%d_%d" % (l, i), [128, KC, 512], BF16, es) for i in range(2)]
        Wdn = [P.sb("Wdn%d_%d" % (l, i), [128, 4, D], BF16, es) for i in range(2)]
        hid = P.sb("hid%d" % l, [128, 4, 512], BF16, es)
        rl = P.sb("rl%d" % l, [128, 512], BF16, es)
        for i, t in enumerate(tiles):
            rms_rows(hres[:, t, :], "hres%d" % t, xh2[:], "xh2", 128, xh2, ss2, rs2, jname="xh2")
            for c in range(KC):
                P.tr(psT[:, c * 128:(c + 1) * 128], xh2[:, c * 128:(c + 1) * 128], ident_b[:], ["xh2", "ident_b"], ["psT"], inc=(c == KC - 1))
            P.cp("act", hnA[:, :, i * 128:(i + 1) * 128], psT[:].rearrange("p (c t) -> p c t", c=KC), ["psT"], ["hnA"])
        groups = [list(range(i, min(i + 4, nt))) for i in range(0, nt, 4)]
        for f in range(8):
            wu, wd = Wup[f % 2], Wdn[f % 2]
            wun, wdn = "Wup%d" % (f % 2), "Wdn%d" % (f % 2)
            load_w(w_up[l][:, f * 512:(f + 1) * 512], D, 512, wu, gain=gains[:, 2 + l, :], dst_name=wun)
            load_w(w_down[l][f * 512:(f + 1) * 512, :], 512, D, wd, dst_name=wdn)
            for grp in groups:
                ntok = len(grp) * 128
                t0 = grp[0] * 128
                for fc in range(4):
                    pz = psA if fc % 2 == 0 else psG
                    pzn = "psA" if fc % 2 == 0 else "psG"
                    for k in range(KC):
                        P.mm(pz[:, 0:ntok], wu[:, k, fc * 128:(fc + 1) * 128], hnA[:, k, t0:t0 + ntok], [wun, "hnA"], [pzn], start=(k == 0), stop=(k == KC - 1), inc=(k == KC - 1))
                    P.act(rl[:, 0:ntok], pz[:, 0:ntok], AF.Relu, [pzn], ["rl"])
                    P.tt("pool" if fc % 2 else "dve", hid[:, fc, 0:ntok], rl[:, 0:ntok], rl[:, 0:ntok], ALU.mult, ["rl"], ["hid"])
                for j, ti_ in enumerate(grp):
                    t = tiles[ti_]
                    for half in range(2):
                        po = psW if half == 0 else psD
                        pon = "psW" if half == 0 else "psD"
                        for fc in range(4):
                            P.mm(po[:, :], hid[:, fc, j * 128:(j + 1) * 128], wd[:, fc, half * 512:(half + 1) * 512], ["hid", wdn], [pon], start=(fc == 0), stop=(fc == 3), inc=(fc == 3))
                        P.tt("dve", hres[:, t, half * 512:(half + 1) * 512], hres[:, t, half * 512:(half + 1) * 512], po[:, :], ALU.add, ["hres%d" % t, pon], ["hres%d" % t])
        P.barrier()
        es.close()

    mlp_phase(0, list(range(NTL)))
    ckpt("C")

    esE = P.scope()
    Wkv = P.sb("Wkv", [128, KC, 768], BF16, esE)
    Winb = P.sb("Winb", [128, KC, D], BF16, esE)
    Wout1 = P.sb("Wout1", [128, KC, D], BF16, esE)
    load_w(w_kv[:, 0:512], D, 512, Wkv, gain=gains[:, 4, :], dst_name="Wkv")
    for k in range(KC):
        for (d0, s0) in ((512, 64), (576, 0), (640, 192), (704, 128)):
            P.cp("pool", Wkv[:, k, d0:d0 + 64], Wkv[:, k, s0:s0 + 64], ["Wkv"], ["Wkv"])
    load_w(w_in_b[0], D, D, Winb, gain=gains[:, 1, :], dst_name="Winb")
    load_w(w_out[1], D, D, Wout1, dst_name="Wout1")
    hnT2 = P.sb("hnT2", [128, KC, 128], BF16, esE)
    KF = P.sb("KF", [128, 6, 128], F32, esE)
    KFb = P.sb("KFb", [128, 6, 128], BF16, esE)
    KR = P.sb("KR", [128, 6, 128], F32, esE)
    QF = P.sb("QF", [128, 6, 128], BF16, esE)
    QM2 = P.sb("QM2", [128, 2, 128], F32, esE)
    KSTz = [P.sb("KSTz%d" % e, [128, 4, 2, 128], BF16, esE) for e in range(2)]
    VS1 = P.sb("VS1", [128, 2, 4, 66], BF16, esE)
    cs = P.sb("cs", [128, 2, 128], F32, esE)
    ang = P.sb("ang", [128, 128], F32, esE)
    angi = P.sb("angi", [128, 128], I32, esE)
    posr = P.sb("posr", [128, 128], F32, esE)
    sqb2 = P.sb("sqb_2", [128, 6, 128], BF16, esE)
    T4b = P.sb("T4_2", [128, 6, 128], F32, esE)
    PT2 = P.sb("PT2", [128, 8, 128], BF16, esE)
    cat2 = P.sb("cat_2", [128, D], BF16, esE)
    catT2 = P.sb("catT_2", [128, KC, 128], BF16, esE)
    QN2 = P.sb("QN_2", [128, 2, 128], BF16, esE)
    PTm2 = P.sb("PTm_2", [128, 8, 128], BF16, esE)
    mor2 = P.sb("mor_2", [128, 12], F32, esE)
    vout = P.sb("vout", [128, 512], F32, esE)
    for e in range(2):
        P.memset("pool", KSTz[e][:], 0.0, ["KSTz%d_0" % e, "KSTz%d_1" % e])
    P.memset("pool", VS1[:], 1.0, ["VS1_0", "VS1_1"])
    P.op("pool", lambda e: e.iota(angi[:], [[1, 128]], base=0, channel_multiplier=0), (), ["angi"])
    P.cp("pool", posr[:], angi[:], ["angi"], ["posr"])
    P.ts("dve", posr[:], posr[:], p0t[:, 0:1], None, ALU.add, None, ["posr", "p0t"], ["posr"])

    def rope_tables(t):
        for j, shift in enumerate((np.pi / 2, 0.0)):
            P.ts("dve", ang[:], posr[:], float(t * 128), frq[:, 0:1], ALU.add, ALU.mult, ["posr", "frq"], ["ang"])
            if shift:
                P.ts("dve", ang[:], ang[:], float(shift), None, ALU.add, None, ["ang"], ["ang"])
            P.ts("dve", cs[:, j, :], ang[:], float(1.0 / (2 * np.pi)), None, ALU.mult, None, ["ang"], ["cs"])
            P.cp("dve", angi[:], cs[:, j, :], ["cs"], ["angi"])
            P.cp("dve", cs[:, j, :], angi[:], ["angi"], ["cs"])
            P.stt("dve", ang[:], cs[:, j, :], float(-2 * np.pi), ang[:], ALU.mult, ALU.add, ["cs", "ang"], ["ang"])
            P.act(cs[:, j, :], ang[:], AF.Sin, ["ang"], ["cs"])

    def headnorm_rope(nch, gcol):
        P.act(sqb2[:, 0:nch, :], KF[:, 0:nch, :], AF.Square, ["KF"], ["sqb"])
        for c in range(nch):
            pz, pzn = (psA, "psA") if c < 4 else (psG, "psG")
            P.mm(pz[:, (c % 4) * 128:(c % 4 + 1) * 128], bones[:], sqb2[:, c, :], ["bones", "sqb"], [pzn], inc=(c in (3, nch - 1)))
        n1 = min(nch, 4)
        P.act(T4b[:, 0:n1, :], psA[:, 0:n1 * 128].rearrange("p (c t) -> p c t", c=n1), AF.Sqrt, ["psA", "c_eps6"], ["T4"], bias=c_eps6[:], scale=1.0 / 64)
        if nch > 4:
            P.act(T4b[:, 4:nch, :], psG[:, 0:(nch - 4) * 128].rearrange("p (c t) -> p c t", c=nch - 4), AF.Sqrt, ["psG", "c_eps6"], ["T4"], bias=c_eps6[:], scale=1.0 / 64)
        P.recip(T4b[:, 0:nch, :], T4b[:, 0:nch, :], ["T4"], ["T4"])
        P.stt("dve", KF[:, 0:nch, :], KF[:, 0:nch, :], gcol, T4b[:, 0:nch, :], ALU.mult, ALU.mult, ["KF", "T4", "gsq", "gsk"], ["KF"])
        P.cp("act", KFb[:, 0:nch, :], KF[:, 0:nch, :], ["KF"], ["KFb"])
        for c in range(nch):
            pz, pzn = (psA, "psA") if c < 4 else (psG, "psG")
            P.mm(pz[:, (c % 4) * 128:(c % 4 + 1) * 128], Rrot[:], KFb[:, c, :], ["Rrot", "KFb"], [pzn], inc=(c in (3, nch - 1)))
        P.tt("dve", KR[:, 0:n1, :], psA[:, 0:n1 * 128].rearrange("p (c t) -> p c t", c=n1), bc(cs[:, 1, :].unsqueeze(1), [128, n1, 128]), ALU.mult, ["psA", "cs"], ["KR"])
        if nch > 4:
            P.tt("dve", KR[:, 4:nch, :], psG[:, 0:(nch - 4) * 128].rearrange("p (c t) -> p c t", c=nch - 4), bc(cs[:, 1, :].unsqueeze(1), [128, nch - 4, 128]), ALU.mult, ["psG", "cs"], ["KR"])
        P.tt("dve", KF[:, 0:nch, :], KF[:, 0:nch, :], bc(cs[:, 0, :].unsqueeze(1), [128, nch, 128]), ALU.mult, ["KF", "cs"], ["KF"])
        P.tt("dve", KR[:, 0:nch, :], KR[:, 0:nch, :], KF[:, 0:nch, :], ALU.add, ["KR", "KF"], ["KR"])

    def fm_proj(W, wname, col_chunks, dst3, dname):
        for j, cc in enumerate(col_chunks):
            pz, pzn = (psA, "psA") if j < 4 else (psG, "psG")
            for k in range(KC):
                P.mm(pz[:, (j % 4) * 128:(j % 4 + 1) * 128], W[:, k, cc * 128:(cc + 1) * 128], hnT2[:, k, :], [wname, "hnT2"], [pzn], start=(k == 0), stop=(k == KC - 1),
                     inc=(k == KC - 1 and (j == 3 or j == len(col_chunks) - 1)))
        n1 = min(len(col_chunks), 4)
        P.cp("act", dst3[:, 0:n1, :], psA[:, 0:n1 * 128].rearrange("p (c t) -> p c t", c=n1), ["psA"], [dname])
        if len(col_chunks) > 4:
            n2 = len(col_chunks) - 4
            P.cp("act", dst3[:, 4:4 + n2, :], psG[:, 0:n2 * 128].rearrange("p (c t) -> p c t", c=n2), ["psG"], [dname])

    for t in range(NMAIN):
        slot = t % 2
        hn = "hres%d" % t
        rope_tables(t)
        rms_rows(hres[:, t, :], hn, xh2[:], "xh2", 128, xh2, ss2, rs2, jname="xh2")
        to_fm(xh2, "xh2", 128, KC, hnT2, "hnT2")
        fm_proj(Wkv, "Wkv", [0, 1, 4, 5], KF, "KF")
        headnorm_rope(4, gsk[:, 0:1])
        for (ch, hs, kvh, e) in ((0, 0, 0, 0), (2, 0, 1, 0), (1, 0, 2, 0), (3, 0, 3, 0), (2, 1, 0, 1), (0, 1, 1, 1), (3, 1, 2, 1), (1, 1, 3, 1)):
            sl = slice(hs * 64, (hs + 1) * 64)
            P.cp("act" if e == 0 else "pool", KSTz[e][sl, kvh, slot, :], KR[sl, ch, :], ["KR"], ["KSTz%d_%d" % (e, slot)])
        for k in range(KC):
            P.mm(psW[:, 0:256], hnT2[:, k, :], Wkv[:, k, 256:512], ["hnT2", "Wkv"], ["psW"], start=(k == 0), stop=(k == KC - 1), inc=(k == KC - 1))
        P.cp("act", vout[:, 256:512], psW[:, 0:256], ["psW"], ["vout"])
        P.cp("pool", VS1[:, slot, :, 0:64], vout[:, 256:512].rearrange("p (h d) -> p h d", h=4), ["vout"], ["VS1_%d" % slot])
        if t == NMAIN - 1:
            for c in range(2):
                P.op("pe", lambda e, c=c: e.transpose(psD[:, c * 128:(c + 1) * 128], KR[:, c, :], ident_f[:]), ["KR", "ident_f"], ["psD"])
            P.cp("act", vout[:, 0:256], psD[:, 0:256], ["psD"], ["vout"])
            P.dma(o_pk[:, :], vout[:, 0:256], reads=["vout"])
            P.dma(o_pv[:, :], vout[:, 256:512], reads=["vout"])
        if t == 0:
            continue
        pslot = (t - 1) % 2
        fm_proj(Winb, "Winb", [0, 1, 2, 3, 4, 5], KF, "KF")
        headnorm_rope(6, gsq[:, 0:1])
        P.cp("act", QF[:], KR[:], ["KR"], ["QF"])
        fm_proj(Winb, "Winb", [6, 7], QM2, "QM")
        mprev = MPf if t == 1 else MPm
        for hg in range(3):
            for hl in range(4):
                h = hg * 4 + hl
                for kb, sl_ in enumerate((pslot, slot)):
                    pz, pzn = (psY0, "psY0") if hl < 2 else (psY1, "psY1")
                    col = ((hl % 2) * 2 + kb) * 128
                    P.mm(pz[:, col:col + 128], KSTz[h % 2][:, h // 3, sl_, :], QF[:, h // 2, :], ["KSTz%d_%d" % (h % 2, sl_), "QF"], [pzn], inc=(kb == 1 and hl % 2 == 1))
            P.act(PT2[:, 0:4, :], psY0[:].rearrange("p (c t) -> p c t", c=4), AF.Exp, ["psY0"], ["PT2"])
            P.act(PT2[:, 4:8, :], psY1[:].rearrange("p (c t) -> p c t", c=4), AF.Exp, ["psY1"], ["PT2"])
            pv = PT2[:].rearrange("p (h k) t -> p h k t", k=2)
            P.tt("dve", pv[:, :, 0, :], pv[:, :, 0, :], bc(mprev[:].unsqueeze(1), [128, 4, 128]), ALU.mult, ["PT2", "MPm", "MPf"], ["PT2"])
            P.tt("dve", pv[:, :, 1, :], pv[:, :, 1, :], bc(MC[:].unsqueeze(1), [128, 4, 128]), ALU.mult, ["PT2", "MC"], ["PT2"])
            for hl in range(4):
                h = hg * 4 + hl
                pz, pzn = (psW, "psW") if h < 6 else (psD, "psD")
                for kb, sl_ in enumerate((pslot, slot)):
                    P.mm(pz[:, (h % 6) * 66:(h % 6) * 66 + 65], PT2[:, hl * 2 + kb, :], VS1[:, sl_, h // 3, 0:65], ["PT2", "VS1_%d" % sl_], [pzn], start=(kb == 0), stop=(kb == 1), inc=(kb == 1))
        for half in range(2):
            pz, pzn = (psW, "psW") if half == 0 else (psD, "psD")
            pd3 = pz[:, 0:396].rearrange("p (h c) -> p h c", h=6)
            P.tt("dve", mor2[:, half * 6:(half + 1) * 6], pd3[:, :, 64], esink[:, half * 6:(half + 1) * 6], ALU.add, [pzn, "esink"], ["mor"])
            P.recip(mor2[:, half * 6:(half + 1) * 6], mor2[:, half * 6:(half + 1) * 6], ["mor"], ["mor"])
            P.tt("dve", cat2[:, half * 384:(half + 1) * 384].rearrange("p (h d) -> p h d", h=6), pd3[:, :, 0:64], bc(mor2[:, half * 6:(half + 1) * 6].unsqueeze(2), [128, 6, 64]), ALU.mult, [pzn, "mor"], ["cat"])
        qnorm_l1 = QM2
        P.act(sqb2[:, 0:2, :], QM2[:], AF.Square, ["QM"], ["sqb"])
        for c in range(2):
            P.mm(psA[:, c * 128:(c + 1) * 128], bones[:], sqb2[:, c, :], ["bones", "sqb"], ["psA"], inc=(c == 1))
        P.act(T4b[:, 0:2, :], psA[:, 0:256].rearrange("p (c t) -> p c t", c=2), AF.Sqrt, ["psA", "c_eps6"], ["T4"], bias=c_eps6[:], scale=1.0 / 64)
        P.recip(T4b[:, 0:2, :], T4b[:, 0:2, :], ["T4"], ["T4"])
        P.stt("dve", QN2[:], QM2[:], gmq[:, 1:2], T4b[:, 0:2, :], ALU.mult, ALU.mult, ["QM", "T4", "gmq"], ["QN"])
        for h in range(4):
            e_, gp = h % 2, h // 2
            for mc in range(2):
                pz, pzn = (psG, "psG") if h < 2 else (psP, "psP")
                col = ((h % 2) * 2 + mc) * 128
                P.mm(pz[:, col:col + 128], KTmz[e_][:, 1, gp, mc * 128:(mc + 1) * 128], QN2[:, gp, :], ["KTmz%d" % e_, "QN"], [pzn], inc=(mc == 1 and h % 2 == 1))
        P.act(PTm2[:, 0:4, :], psG[:].rearrange("p (c t) -> p c t", c=4), AF.Exp, ["psG"], ["PTm"])
        P.act(PTm2[:, 4:8, :], psP[:].rearrange("p (c t) -> p c t", c=4), AF.Exp, ["psP"], ["PTm"])
        for h in range(4):
            for mc in range(2):
                P.mm(psA[:, h * 66:h * 66 + 65], PTm2[:, h * 2 + mc, :], V1m[:, 1, mc, h, 0:65], ["PTm", "V1m"], ["psA"], start=(mc == 0), stop=(mc == 1), inc=(h == 3 and mc == 1))
        pdm = psA[:, 0:264].rearrange("p (h c) -> p h c", h=4)
        P.recip(mor2[:, 0:4], pdm[:, :, 64], ["psA"], ["mor"])
        P.tt("dve", cat2[:, 768:1024].rearrange("p (h d) -> p h d", h=4), pdm[:, :, 0:64], bc(mor2[:, 0:4].unsqueeze(2), [128, 4, 64]), ALU.mult, ["psA", "mor"], ["cat"])
        to_fm(cat2, "cat", 128, KC, catT2, "catT")
        for half in range(2):
            pz, pzn = (psA, "psA") if half == 0 else (psG, "psG")
            for k in range(KC):
                P.mm(pz[:, :], catT2[:, k, :], Wout1[:, k, half * 512:(half + 1) * 512], ["catT", "Wout1"], [pzn], start=(k == 0), stop=(k == KC - 1), inc=(k == KC - 1))
            P.tt("dve", hres[:, t, half * 512:(half + 1) * 512], hres[:, t, half * 512:(half + 1) * 512], pz[:, :], ALU.add, [hn, pzn], [hn])
    ts_ = NMAIN
    hn = "hres%d" % ts_
    OH = P.sb("OH_2", [64, 64, 64], BF16, esE)
    qtm = P.sb("qtm_2", [64, 256], BF16, esE)
    Vc1 = P.sb("Vc1_2", [128, 2, 4, 66], BF16, esE)
    sTm = P.sb("sTm_2", [128, 2, 4, 4], F32, esE)
    accm = P.sb("accm_2", [64, 4, 66], F32, esE)
    mor = P.sb("mor4", [128, 4], F32, esE)
    X = [None, P.sb("X1_2", [128, 512], F32, esE)]
    LPm = PTm2[:].rearrange("p a t -> p (a t)")[:, 0:512].rearrange("p (m h t) -> p m h t", m=2, h=4)
    qtm2 = P.sb("qtm2", [128, 768], BF16, esE)
    Vc1s = P.sb("Vc1s", [128, 4, 66], BF16, esE)
    Vn1 = P.sb("Vn1", [64, 4, 66], BF16, esE)
    sTs = P.sb("sTs", [128, 4, 12], F32, esE)
    sTe = P.sb("sTe", [128, 4, 12], F32, esE)
    LPc = P.sb("LPc", [128, 12, 64], BF16, esE)
    LPN = P.sb("LPN", [64, 12, 64], BF16, esE)
    accs = P.sb("accs", [64, 12, 66], F32, esE)
    SelT = P.sb("SelT", [64, 4, 64], BF16, esE)
    prods = T4b[:].rearrange("p g t -> p (g t)").rearrange("p (h d) -> p h d", h=12)
    Kcs = stg[0][:, 0:256]
    Vcs = stg[0][:, 256:512]
    P.cp("dve", OH[:], bc(ident_f[0:64, 0:64].unsqueeze(2), [64, 64, 64]), ["ident_f"], ["OH"])
    P.memset("pool", Vc1[:], 1.0, ["Vc1"])
    P.memset("pool", Vc1s[:], 1.0, ["Vc1s"])
    P.memset("pool", Vn1[:], 1.0, ["Vn1"])
    P.memset("pool", LPm, 0.0, ["PTm"])
    P.memset("pool", LPc[:], 0.0, ["LPc"])
    P.memset("pool", accs[:], 0.0, ["accs"])
    for t in range(4):
        P.op("pool", lambda e, t=t: e.iota(tmpi[0:64, 0:64].rearrange("p (a b) -> p a b", a=4), [[0, 4], [-1, 16]], base=-16 * t, channel_multiplier=1), ["tmpi"], ["tmpi"])
        P.cp("dve", tmpf[0:64, 0:64], tmpi[0:64, 0:64], ["tmpi", "tmpf"], ["tmpf"])
        P.ts("dve", SelT[:, t, :], tmpf[0:64, 0:64], 0.0, None, ALU.is_equal, None, ["tmpf"], ["SelT"])
    P.op("pool", lambda e: e.iota(angi[:, 0:64].rearrange("p (a b) -> p a b", a=4), [[1, 4], [0, 16]], base=PAST_LEN, channel_multiplier=0), ["angi"], ["angi"])
    P.memset("pool", posr[:], 0.0, ["posr"])
    P.cp("pool", posr[:, 0:64], angi[:, 0:64], ["angi", "posr"], ["posr"])
    rope_tables(0)
    rms_rows(hres[:, ts_, :], hn, xh2[:], "xh2", 128, xh2, ss2, rs2, jname="xh2")
    to_fm(xh2, "xh2", 128, KC, hnT2, "hnT2")
    fm_proj(Wkv, "Wkv", [0, 1, 4, 5], KF, "KF")
    headnorm_rope(4, gsk[:, 0:1])
    for c in range(2):
        P.op("pe", lambda e, c=c: e.transpose(psD[:, c * 128:(c + 1) * 128], KR[:, c, :], ident_f[:]), ["KR", "ident_f"], ["psD"])
    P.cp("act", vout[:, 0:256], psD[:, 0:256], ["psD"], ["vout"])
    for k in range(KC):
        P.mm(psW[:, 0:256], hnT2[:, k, :], Wkv[:, k, 256:512], ["hnT2", "Wkv"], ["psW"], start=(k == 0), stop=(k == KC - 1), inc=(k == KC - 1))
    P.cp("act", vout[:, 256:512], psW[:, 0:256], ["psW"], ["vout"])
    P.cp("pool", Vn1[:, :, 0:64], vout[0:64, 256:512].rearrange("p (h d) -> p h d", h=4), ["vout"], ["Vn1"])
    P.dma(o_sk[:, 0:124, :], sck[:, 4:128, :])
    P.dma(o_sv[:, 0:124, :], scv[:, 4:128, :])
    for t in range(4):
        P.dma(o_sk[:, 124 + t, :], vout[t * 16:(t + 1) * 16, 0:256], reads=["vout"])
        P.dma(o_sv[:, 124 + t, :], vout[t * 16:(t + 1) * 16, 256:512], reads=["vout"])
    fm_proj(Winb, "Winb", [0, 1, 2, 3, 4, 5], KF, "KF")
    headnorm_rope(6, gsq[:, 0:1])
    P.cp("act", QF[:], KR[:], ["KR"], ["QF"])
    fm_proj(Winb, "Winb", [6, 7], QM2, "QM")
    for g in range(NG):
        P.tr(psT[:, g * 128:(g + 1) * 128], QF[:, g, :], ident_b[:], ["QF", "ident_b"], ["psT"], inc=(g == NG - 1))
    P.cp("act", qtm2[:], psT[:, 0:768], ["psT"], ["qtm2"])
    pcs = ((psY0, "psY0"), (psY1, "psY1"))
    for b in range(NB):
        P.dma(Kcs, sck[b], writes=["stg0"])
        P.dma(Vcs, scv[b], writes=["stg0"])
        P.cp("pool", Vc1s[:, :, 0:64], Vcs.rearrange("p (h d) -> p h d", h=4), ["stg0"], ["Vc1s"])
        for t in range(4):
            k0 = t * 16 + b
            for piece, (pz, pzn) in enumerate(pcs):
                for e in range(2):
                    P.mm(pz[e * 64:(e + 1) * 64, 0:384], OH[:, k0, :], qtm2[0:64, piece * 384:(piece + 1) * 384], ["OH", "qtm2"], [pzn], inc=(e == 1))
            for piece, (pz, pzn) in enumerate(pcs):
                for k2 in range(2):
                    kvh = piece * 2 + k2
                    P.tt("dve", prods[:, kvh * 3:(kvh + 1) * 3, :], pz[:, k2 * 192:(k2 + 1) * 192].rearrange("p (h d) -> p h d", h=3),
                         bc(Kcs[:, kvh * 64:(kvh + 1) * 64].unsqueeze(1), [128, 3, 64]), ALU.mult, [pzn, "stg0"], ["T4"])
            P.reduce(sTs[:, t, :], prods, ["T4"], ["sTs"])
        P.act(sTe[:], sTs[:], AF.Exp, ["sTs"], ["sTe"])
        P.tt("dve", LPc[:, :, b:64:16], sTe[:].rearrange("p t h -> p h t"), bc(MSC[:].unsqueeze(1), [128, 12, 4]), ALU.mult, ["sTe", "MSC"], ["LPc"])
        for h in range(12):
            pz, pzn = (psW, "psW") if h < 6 else (psD, "psD")
            P.mm(pz[0:64, (h % 6) * 66:(h % 6) * 66 + 65], LPc[:, h, :], Vc1s[:, h // 3, 0:65], ["LPc", "Vc1s"], [pzn], inc=(h in (5, 11)))
        P.tt("dve", accs[:, 0:6, :], accs[:, 0:6, :], psW[0:64, 0:396].rearrange("p (h c) -> p h c", h=6), ALU.add, ["accs", "psW"], ["accs"])
        P.tt("dve", accs[:, 6:12, :], accs[:, 6:12, :], psD[0:64, 0:396].rearrange("p (h c) -> p h c", h=6), ALU.add, ["accs", "psD"], ["accs"])
        P.memset("pool", LPc[:, :, b:64:16], 0.0, ["LPc"])
    for t in range(4):
        for piece, (pz, pzn) in enumerate(pcs):
            P.mm(pz[0:64, 0:384], SelT[:, t, :], qtm2[0:64, piece * 384:(piece + 1) * 384], ["SelT", "qtm2"], [pzn])
        for piece, (pz, pzn) in enumerate(pcs):
            for k2 in range(2):
                kvh = piece * 2 + k2
                P.tt("dve", prods[0:64, kvh * 3:(kvh + 1) * 3, :], pz[0:64, k2 * 192:(k2 + 1) * 192].rearrange("p (h d) -> p h d", h=3),
                     bc(vout[0:64, kvh * 64:(kvh + 1) * 64].unsqueeze(1), [64, 3, 64]), ALU.mult, [pzn, "vout"], ["T4"])
        P.reduce(sTs[0:64, t, :], prods[0:64], ["T4"], ["sTs"])
    P.act(sTe[0:64], sTs[0:64], AF.Exp, ["sTs"], ["sTe"])
    P.tt("dve", sTe[0:64], sTe[0:64], bc(MN[:].unsqueeze(2), [64, 4, 12]), ALU.mult, ["sTe", "MN"], ["sTe"])
    for h in range(12):
        P.tt("dve", LPN[:, h, :].rearrange("p (a b) -> p a b", a=4), SBm[:].rearrange("p (a b) -> p a b", a=4), bc(sTe[0:64, :, h:h + 1], [64, 4, 16]), ALU.mult, ["SBm", "sTe"], ["LPN"])
    for h in range(12):
        pz, pzn = (psW, "psW") if h < 6 else (psD, "psD")
        P.mm(pz[0:64, (h % 6) * 66:(h % 6) * 66 + 65], LPN[:, h, :], Vn1[:, h // 3, 0:65], ["LPN", "Vn1"], [pzn], inc=(h in (5, 11)))
    P.tt("dve", accs[:, 0:6, :], accs[:, 0:6, :], psW[0:64, 0:396].rearrange("p (h c) -> p h c", h=6), ALU.add, ["accs", "psW"], ["accs"])
    P.tt("dve", accs[:, 6:12, :], accs[:, 6:12, :], psD[0:64, 0:396].rearrange("p (h c) -> p h c", h=6), ALU.add, ["accs", "psD"], ["accs"])
    P.tt("dve", mor2[0:64, :], accs[:, :, 64], esink[0:64, :], ALU.add, ["accs", "esink"], ["mor"])
    P.recip(mor2[0:64, :], mor2[0:64, :], ["mor"], ["mor"])
    P.tt("dve", cat2[0:64, 0:768].rearrange("p (h d) -> p h d", h=12), accs[:, :, 0:64], bc(mor2[0:64, :].unsqueeze(2), [64, 12, 64]), ALU.mult, ["accs", "mor"], ["cat"])
    P.act(sqb2[:, 0:2, :], QM2[:], AF.Square, ["QM"], ["sqb"])
    for c in range(2):
        P.mm(psA[:, c * 128:(c + 1) * 128], bones[:], sqb2[:, c, :], ["bones", "sqb"], ["psA"], inc=(c == 1))
    P.act(T4b[:, 0:2, :], psA[:, 0:256].rearrange("p (c t) -> p c t", c=2), AF.Sqrt, ["psA", "c_eps6"], ["T4"], bias=c_eps6[:], scale=1.0 / 64)
    P.recip(T4b[:, 0:2, :], T4b[:, 0:2, :], ["T4"], ["T4"])
    P.stt("dve", QN2[:], QM2[:], gmq[:, 1:2], T4b[:, 0:2, :], ALU.mult, ALU.mult, ["QM", "T4", "gmq"], ["QN"])
    mem_attn_sample_impl(1, QN2, "QN", cat2, "cat")
    to_fm(cat2, "cat", 128, KC, catT2, "catT")
    for half in range(2):
        pz, pzn = (psA, "psA") if half == 0 else (psG, "psG")
        for k in range(KC):
            P.mm(pz[:, :], catT2[:, k, :], Wout1[:, k, half * 512:(half + 1) * 512], ["catT", "Wout1"], [pzn], start=(k == 0), stop=(k == KC - 1), inc=(k == KC - 1))
        P.tt("dve", hres[:, ts_, half * 512:(half + 1) * 512], hres[:, ts_, half * 512:(half + 1) * 512], pz[:, :], ALU.add, [hn, pzn], [hn])
    P.barrier()
    esE.close()
    ckpt("E")
    mlp_phase(1, list(range(1, NTL)))
    for t in range(1, NMAIN):
        P.dma(o_yp[(t - 1) * 128:t * 128, :], hres[:, t, :], reads=["hres%d" % t])
    P.dma(o_ys[:, :], hres[0:64, NMAIN, :], reads=["hres%d" % NMAIN])
    return nc, P, dict(NT=NT, NPRE=NPRE, NMAIN=NMAIN, NTL=NTL)


WEIGHT_NAMES = ["norm_mix", "norm_mlp", "w_out", "w_up", "w_down", "mem_norm", "w_mem_kv", "mem_qnorm", "mem_knorm",
                "w_in_a", "shift_mu", "w_w2", "w0", "w_a2", "a0", "w_g2", "k_k", "k_a", "r_k", "lnx_w", "lnx_b",
                "w_in_b", "swa_qnorm", "sinks", "kv_norm", "w_kv", "swa_knorm"]


def make_in_maps(inp, SEGT):
    SEG = SEGT * 128
    SEQ = 4 * SEG
    f = lambda a: np.ascontiguousarray(np.asarray(a, dtype=np.float32))
    xp = f(inp["x_prompt"])
    assert xp.shape == (2, SEQ, D)
    xsamp = f(inp["x_sample"])
    maps = []
    for c in range(8):
        b, q = c // 4, c % 4
        s = q * SEG
        xall = np.zeros((SEQ, D), np.float32)
        xall[SEQ - (s + SEG):] = xp[b, :s + SEG]
        sl = slice(16 * c, 16 * c + 16)
        m = {
            "xall": xall,
            "xs": np.ascontiguousarray(xsamp[sl].transpose(1, 0, 2).reshape(64, D)),
            "sshift": f(inp["state_rwkv_shift"])[0, sl],
            "swkv": f(inp["state_rwkv_wkv"])[0, sl],
            "sck": f(inp["cache_swa_k"])[sl].reshape(16, 128, 256),
            "scv": f(inp["cache_swa_v"])[sl].reshape(16, 128, 256),
            "smk": f(inp["cache_mem_k"])[:, sl].reshape(2, 16, 256, 256),
            "smv": f(inp["cache_mem_v"])[:, sl].reshape(2, 16, 256, 256),
            "memp": f(inp["mem_prompt"])[b],
            "pos0": np.full((128, 1), float(s - 128), np.float32),
            "hflag": np.full((128, 1), 1.0 if q > 0 else 0.0, np.float32),
        }
        for w in WEIGHT_NAMES:
            m[w] = f(inp[w])
        maps.append({k: np.ascontiguousarray(v) for k, v in m.items()})
    return maps


SEGT_FULL = 16


def kernel(**inputs):
    SEGT = SEGT_FULL
    SEG = SEGT * 128
    nc, P, info = build_program(SEGT)
    P.emit()
    maps = make_in_maps(inputs, SEGT)
    res = run_bass_kernel_spmd(nc, maps, core_ids=list(range(8)))
    r = res.results
    f32 = np.float32
    y_prompt = np.stack([np.concatenate([np.asarray(r[4 * b + q]["o_yp"], f32) for q in range(4)], axis=0) for b in range(2)])
    ys = np.stack([np.asarray(r[c]["o_ys"], f32).reshape(4, 16, D).transpose(1, 0, 2) for c in range(8)]).reshape(128, 4, D)
    p_shift = np.stack([np.asarray(r[4 * b + 3]["o_pshift"], f32).reshape(RW) for b in range(2)])[None]
    p_wkv = np.stack([np.asarray(r[4 * b + 3]["o_pwkv"], f32) for b in range(2)])[None]
    p_k = np.stack([np.asarray(r[4 * b + 3]["o_pk"], f32).reshape(128, 4, 64) for b in range(2)])
    p_v = np.stack([np.asarray(r[4 * b + 3]["o_pv"], f32).reshape(128, 4, 64) for b in range(2)])
    p_mk = np.stack([np.asarray(r[4 * b]["o_pmk"], f32).reshape(2, 256, 4, 64) for b in range(2)], axis=1)
    p_mv = np.stack([np.asarray(r[4 * b]["o_pmv"], f32).reshape(2, 256, 4, 64) for b in range(2)], axis=1)
    s_shift = np.concatenate([np.asarray(r[c]["o_sshift"], f32) for c in range(8)], axis=0)[None]
    s_wkv = np.concatenate([np.asarray(r[c]["o_swkv"], f32) for c in range(8)], axis=0)[None]
    s_k = np.concatenate([np.asarray(r[c]["o_sk"], f32) for c in range(8)], axis=0).reshape(128, 128, 4, 64)
    s_v = np.concatenate([np.asarray(r[c]["o_sv"], f32) for c in range(8)], axis=0).reshape(128, 128, 4, 64)
    return (np.ascontiguousarray(y_prompt), np.ascontiguousarray(ys), np.ascontiguousarray(p_shift), np.ascontiguousarray(p_wkv),
            np.ascontiguousarray(p_k), np.ascontiguousarray(p_v), np.ascontiguousarray(p_mk), np.ascontiguousarray(p_mv),
            np.ascontiguousarray(s_shift), np.ascontiguousarray(s_wkv), np.ascontiguousarray(s_k), np.ascontiguousarray(s_v))
```
